# Optimizing a Trainium2 kernel written in Bass

```python
import jax, jax.numpy as jnp
from jax import lax
import numpy as np

D_MODEL = 4096
BATCH = 32
SEQ = 256
DEPTH = 2
DEC_BATCH = 4
DEC_SEQ = 1024
PAST_LEN = 512

GRID_W = 64
BRANCH_W = 2048
N_BRANCH = 3
EPS = 1e-6
ATT_HEADS = 16
ATT_KV_HEADS = 4
HEAD_DIM = 128
Q_BLOCK = 128
ROPE_THETA = 10000.0
SSM_HEAD_DIM = 64
SSM_HEADS = BRANCH_W // SSM_HEAD_DIM
SSM_GROUPS = 4
SSM_STATE = 128
SSM_CHUNK = 64
CONV_K = 3
SSM_BC = SSM_GROUPS * SSM_STATE
SSM_CONV_DIM = BRANCH_W + 2 * SSM_BC
HG_EXPAND = 128
HG_HEADS = BRANCH_W // HG_EXPAND
HG_HEAD_V = BRANCH_W // HG_HEADS
HG_CHUNK = 32
ATT_Q = ATT_HEADS * HEAD_DIM
ATT_KV = ATT_KV_HEADS * HEAD_DIM
IN_SPLITS = (ATT_Q, ATT_KV, ATT_KV, BRANCH_W,
             BRANCH_W, BRANCH_W, SSM_BC, SSM_BC, 2 * SSM_HEADS,
             BRANCH_W, 2 * BRANCH_W, BRANCH_W, BRANCH_W,
             N_BRANCH * D_MODEL)
N_IN = sum(IN_SPLITS)

kernel_name = 'hybrid_diffusion_attn_ssd_hgrn2_step'


def rms_norm(x, w):
    xf = x.astype(jnp.float32)
    y = xf * lax.rsqrt(jnp.mean(xf * xf, axis=-1, keepdims=True) + EPS)
    return (y * w.astype(jnp.float32)).astype(x.dtype)


def flip(a):
    return jnp.flip(a, axis=1)


def axial_angles(t_len):
    rows = t_len // GRID_W
    row = jnp.repeat(jnp.arange(rows), GRID_W).astype(jnp.float32)
    col = jnp.tile(jnp.arange(GRID_W), rows).astype(jnp.float32)
    half = HEAD_DIM // 2
    inv = ROPE_THETA ** (-jnp.arange(0, half, 2, dtype=jnp.float32) / half)
    return row[:, None] * inv, col[:, None] * inv


def rope_rotate(x, ang):
    m = ang.shape[-1]
    cos = jnp.cos(ang)[None, :, None, :]
    sin = jnp.sin(ang)[None, :, None, :]
    x1, x2 = x[..., :m], x[..., m:]
    return jnp.concatenate([x1 * cos - x2 * sin, x2 * cos + x1 * sin], axis=-1)


def axial_rope(x, row_ang, col_ang):
    half = HEAD_DIM // 2
    xf = x.astype(jnp.float32)
    out = jnp.concatenate([rope_rotate(xf[..., :half], row_ang),
                           rope_rotate(xf[..., half:], col_ang)], axis=-1)
    return out.astype(x.dtype)


def block_attention(q, k, v):
    b, t = q.shape[0], q.shape[1]
    nb = t // Q_BLOCK
    grp = ATT_HEADS // ATT_KV_HEADS
    qb = q.reshape(b, nb, Q_BLOCK, ATT_KV_HEADS, grp, HEAD_DIM).transpose(1, 0, 2, 3, 4, 5)
    kf = k.astype(jnp.float32)
    vf = v.astype(jnp.float32)
    scale = HEAD_DIM ** -0.5

    def one_block(qi):
        s = jnp.einsum('bqkgd,bskd->bkgqs', qi.astype(jnp.float32), kf) * scale
        p = jax.nn.softmax(s, axis=-1)
        return jnp.einsum('bkgqs,bskd->bqkgd', p, vf)

    o = lax.map(one_block, qb)
    return o.transpose(1, 0, 2, 3, 4, 5).reshape(b, t, ATT_Q).astype(q.dtype)


def dwconv_centred(x, w, bias):
    ch = x.shape[-1]
    y = lax.conv_general_dilated(x, w[:, None, :], window_strides=(1,),
                                 padding=[(CONV_K // 2, CONV_K // 2)],
                                 dimension_numbers=('NWC', 'WIO', 'NWC'),
                                 feature_group_count=ch)
    return y + bias


def ssd_scan(x, dt, a, bm, cm, s0):
    bsz, t, h, p = x.shape
    g, n = bm.shape[2], bm.shape[3]
    r = h // g
    nc = t // SSM_CHUNK
    L = SSM_CHUNK
    xc = x.astype(jnp.float32).reshape(bsz, nc, L, g, r, p)
    dtc = dt.reshape(bsz, nc, L, g, r)
    bc = bm.astype(jnp.float32).reshape(bsz, nc, L, g, n)
    cc = cm.astype(jnp.float32).reshape(bsz, nc, L, g, n)
    acum = jnp.cumsum(dtc * a.reshape(g, r), axis=2)
    mask = jnp.tril(jnp.ones((L, L), dtype=bool))[None, None, :, :, None, None]
    seg = acum[:, :, :, None] - acum[:, :, None, :]
    decay = jnp.where(mask, jnp.exp(jnp.where(mask, seg, 0.0)), 0.0)
    cb = jnp.einsum('bclgn,bcsgn->bclsg', cc, bc)
    wts = cb[..., None] * decay * dtc[:, :, None]
    y_diag = jnp.einsum('bclsgr,bcsgrp->bclgrp', wts, xc)
    decay_end = jnp.exp(acum[:, :, -1:] - acum) * dtc
    states = jnp.einsum('bcsgn,bcsgr,bcsgrp->bcgrpn', bc, decay_end, xc)
    chunk_decay = jnp.exp(acum[:, :, -1])

    def step(s, inp):
        st, dec = inp
        return dec[..., None, None] * s + st, s

    s_fin, s_start = lax.scan(step, s0.astype(jnp.float32).reshape(bsz, g, r, p, n),
                              (jnp.swapaxes(states, 0, 1), jnp.swapaxes(chunk_decay, 0, 1)))
    y_off = jnp.einsum('bclgn,bcgrpn,bclgr->bclgrp', cc, jnp.swapaxes(s_start, 0, 1), jnp.exp(acum))
    y = (y_diag + y_off).reshape(bsz, t, h, p)
    return y, s_fin.reshape(bsz, h, p, n)


def hgrn_scan(q, k, v, log_f, s0):
    b, t, h, _ = q.shape
    nc = t // HG_CHUNK

    def chunks(a):
        return jnp.swapaxes(a.astype(jnp.float32).reshape(b, nc, HG_CHUNK, *a.shape[2:]), 0, 1)

    mask = jnp.tril(jnp.ones((HG_CHUNK, HG_CHUNK), dtype=bool))[None, :, :, None, None]

    def step(s, inp):
        qc, kc, vc, gc = inp
        cum = jnp.cumsum(gc, axis=1)
        diff = cum[:, :, None] - cum[:, None, :]
        dec = jnp.where(mask, jnp.exp(jnp.where(mask, diff, 0.0)), 0.0)
        att = jnp.einsum('blhk,bshk,blshk->bhls', qc, kc, dec)
        o = (jnp.einsum('bhls,bshv->blhv', att, vc)
             + jnp.einsum('blhk,bhkv->blhv', qc * jnp.exp(cum), s))
        last = cum[:, -1]
        s = (jnp.exp(last)[..., None] * s
             + jnp.einsum('bshk,bshv->bhkv', kc * jnp.exp(last[:, None] - cum), vc))
        return s, o

    s_fin, o = lax.scan(step, s0.astype(jnp.float32), (chunks(q), chunks(k), chunks(v), chunks(log_f)))
    return jnp.swapaxes(o, 0, 1).reshape(b, t, h, -1), s_fin


def trunk_layer(x, mod, lp, lb, pos, ctx):
    (ln_w, w_in, qn_w, kn_w, conv_w, conv_b, dt_bias, a_log, d_skip,
     ssm_nw, hg_nw, w_bout, w_o) = lp
    b, t, _ = x.shape
    shift, scale, gate = jnp.split(mod, 3, axis=-1)
    h = rms_norm(x, ln_w) * (1 + scale[:, None]) + shift[:, None]
    u = h @ w_in
    split_idx = np.cumsum(IN_SPLITS)[:-1].tolist()
    (aq, ak, av, ag, bx, bz, bb, bc, bdt, cq, cf, ci, cg, mg) = jnp.split(u, split_idx, axis=-1)

    q = rms_norm(aq.reshape(b, t, ATT_HEADS, HEAD_DIM), qn_w)
    k = rms_norm(ak.reshape(b, t, ATT_KV_HEADS, HEAD_DIM), kn_w)
    v = av.reshape(b, t, ATT_KV_HEADS, HEAD_DIM)
    if ctx is None:
        k_all, v_all = k, v
    else:
        row_ang, col_ang = pos
        q = axial_rope(q, row_ang, col_ang)
        k_lat = axial_rope(k, row_ang, col_ang)
        k_all = jnp.concatenate([ctx[0].astype(k.dtype), k_lat], axis=1)
        v_all = jnp.concatenate([ctx[1].astype(v.dtype), v], axis=1)
    o_att = block_attention(q, k_all, v_all) * jax.nn.silu(ag)

    xbc = jax.nn.silu(dwconv_centred(jnp.concatenate([bx, bb, bc], axis=-1), conv_w, conv_b))
    sx, sb, sc = jnp.split(xbc, [BRANCH_W, BRANCH_W + SSM_BC], axis=-1)
    sx = sx.reshape(b, t, SSM_HEADS, SSM_HEAD_DIM)
    sb = sb.reshape(b, t, SSM_GROUPS, SSM_STATE)
    sc = sc.reshape(b, t, SSM_GROUPS, SSM_STATE)
    dt = jax.nn.softplus(bdt.reshape(b, t, 2, SSM_HEADS).astype(jnp.float32)
                         + dt_bias.astype(jnp.float32))
    a_neg = -jnp.exp(a_log.astype(jnp.float32))
    if ctx is None:
        s0 = jnp.zeros((b, 2, SSM_HEADS, SSM_HEAD_DIM, SSM_STATE), jnp.float32)
    else:
        s0 = ctx[2]
    y_f, ss_f = ssd_scan(sx, dt[:, :, 0], a_neg[0], sb, sc, s0[:, 0])
    y_b, ss_b = ssd_scan(flip(sx), flip(dt[:, :, 1]), a_neg[1], flip(sb), flip(sc), s0[:, 1])
    y = y_f + flip(y_b) + d_skip.astype(jnp.float32)[None, None, :, None] * sx.astype(jnp.float32)
    y = y.reshape(b, t, BRANCH_W) * jax.nn.silu(bz.astype(jnp.float32))
    o_ssm = rms_norm(y.reshape(b, t, SSM_GROUPS, BRANCH_W // SSM_GROUPS),
                     ssm_nw.reshape(SSM_GROUPS, BRANCH_W // SSM_GROUPS)).reshape(b, t, BRANCH_W)

    hq = jax.nn.silu(cq.reshape(b, t, HG_HEADS, HG_EXPAND).astype(jnp.float32)) * HG_EXPAND ** -0.5
    hv = ci.reshape(b, t, HG_HEADS, HG_HEAD_V)
    zf = cf.reshape(b, t, 2, HG_HEADS, HG_EXPAND).astype(jnp.float32)
    lbr = lb.reshape(2, HG_HEADS, HG_EXPAND)
    log_f = jnp.logaddexp(jnp.log(lbr), jnp.log1p(-lbr) + jax.nn.log_sigmoid(zf))
    hk = (1 - lbr) * jax.nn.sigmoid(-zf)
    if ctx is None:
        h0 = jnp.zeros((b, 2, HG_HEADS, HG_EXPAND, HG_HEAD_V), jnp.float32)
    else:
        h0 = ctx[3]
    o_f, hs_f = hgrn_scan(hq, hk[:, :, 0], hv, log_f[:, :, 0], h0[:, 0])
    o_b, hs_b = hgrn_scan(flip(hq), flip(hk[:, :, 1]), flip(hv), flip(log_f[:, :, 1]), h0[:, 1])
    o_hg = rms_norm(o_f + flip(o_b), hg_nw.reshape(HG_HEADS, HG_HEAD_V)).reshape(b, t, BRANCH_W)
    o_hg = o_hg * jax.nn.silu(cg.astype(jnp.float32))

    gates = jax.nn.sigmoid(mg.reshape(b, t, N_BRANCH, D_MODEL).astype(jnp.float32))
    merged = (gates[:, :, 0] * (o_att.astype(x.dtype) @ w_bout[0]).astype(jnp.float32)
              + gates[:, :, 1] * (o_ssm.astype(x.dtype) @ w_bout[1]).astype(jnp.float32)
              + gates[:, :, 2] * (o_hg.astype(x.dtype) @ w_bout[2]).astype(jnp.float32))
    out = merged.astype(x.dtype) @ w_o
    x_new = x + gate[:, None] * out
    if ctx is not None:
        return x_new
    new_ssm = jnp.stack([ss_f, ss_b], axis=1).astype(x.dtype)
    new_hg = jnp.stack([hs_f, hs_b], axis=1).astype(x.dtype)
    return x_new, (k, v, new_ssm, new_hg)


def setup_inputs(seed: int = 0) -> dict:
    key = jax.random.key(seed)
    ks = jax.random.split(key, 26)

    def nrm(k, shape, s):
        return jax.random.normal(k, shape, jnp.float32) * s

    dt0 = jnp.exp(jax.random.uniform(ks[16], (DEPTH, 2, SSM_HEADS), jnp.float32,
                                     np.log(1e-3), np.log(1e-1)))
    return {
        'x_prompt': nrm(ks[0], (BATCH, SEQ, D_MODEL), 1.0),
        'x_sample': nrm(ks[1], (DEC_BATCH, DEC_SEQ, D_MODEL), 1.0),
        'c': nrm(ks[2], (DEC_BATCH, D_MODEL), 1.0),
        'cache_k': nrm(ks[3], (DEC_BATCH, DEPTH, PAST_LEN, ATT_KV_HEADS, HEAD_DIM), 1.0),
        'cache_v': nrm(ks[4], (DEC_BATCH, DEPTH, PAST_LEN, ATT_KV_HEADS, HEAD_DIM), 1.0),
        'state_ssm': nrm(ks[5], (DEC_BATCH, DEPTH, 2, SSM_HEADS, SSM_HEAD_DIM, SSM_STATE), 0.1),
        'state_hgrn': nrm(ks[6], (DEC_BATCH, DEPTH, 2, HG_HEADS, HG_EXPAND, HG_HEAD_V), 0.5),
        'c_ctx': nrm(ks[7], (D_MODEL,), 1.0),
        'ln_w': 1.0 + nrm(ks[8], (DEPTH, D_MODEL), 0.02),
        'w_mod': nrm(ks[9], (DEPTH, D_MODEL, 3 * D_MODEL), 0.5 * D_MODEL ** -0.5),
        'b_mod': nrm(ks[10], (DEPTH, 3 * D_MODEL), 0.02),
        'w_in': nrm(ks[11], (DEPTH, D_MODEL, N_IN), D_MODEL ** -0.5),
        'q_norm_w': 1.0 + nrm(ks[12], (DEPTH, HEAD_DIM), 0.02),
        'k_norm_w': 1.0 + nrm(ks[13], (DEPTH, HEAD_DIM), 0.02),
        'conv_w': nrm(ks[14], (DEPTH, CONV_K, SSM_CONV_DIM), CONV_K ** -0.5),
        'conv_b': nrm(ks[15], (DEPTH, SSM_CONV_DIM), 0.02),
        'dt_bias': dt0 + jnp.log(-jnp.expm1(-dt0)),
        'a_log': jnp.log(jax.random.uniform(ks[17], (DEPTH, 2, SSM_HEADS), jnp.float32, 1.0, 16.0)),
        'd_skip': 1.0 + nrm(ks[18], (DEPTH, SSM_HEADS), 0.02),
        'ssm_norm_w': 1.0 + nrm(ks[19], (DEPTH, BRANCH_W), 0.02),
        'hgrn_lb': nrm(ks[20], (DEPTH, 2, BRANCH_W), 0.1),
        'hgrn_norm_w': 1.0 + nrm(ks[21], (DEPTH, BRANCH_W), 0.02),
        'w_bout': nrm(ks[22], (DEPTH, N_BRANCH, BRANCH_W, D_MODEL), BRANCH_W ** -0.5),
        'w_o': nrm(ks[23], (DEPTH, D_MODEL, D_MODEL), D_MODEL ** -0.5),
    }


def reference(x_prompt, x_sample, c, cache_k, cache_v, state_ssm, state_hgrn, c_ctx,
              ln_w, w_mod, b_mod, w_in, q_norm_w, k_norm_w, conv_w, conv_b, dt_bias, a_log,
              d_skip, ssm_norm_w, hgrn_lb, hgrn_norm_w, w_bout, w_o):
    lb_p = jax.nn.softmax(hgrn_lb.astype(jnp.float32), axis=0)
    lb_all = jnp.cumsum(lb_p, axis=0) - lb_p[:1]
    pos = axial_angles(x_sample.shape[1])
    xp, xs = x_prompt, x_sample
    ks_l, vs_l, ss_l, hs_l = [], [], [], []
    for l in range(DEPTH):
        lp = (ln_w[l], w_in[l], q_norm_w[l], k_norm_w[l], conv_w[l], conv_b[l], dt_bias[l],
              a_log[l], d_skip[l], ssm_norm_w[l], hgrn_norm_w[l], w_bout[l], w_o[l])
        mod_ctx = jax.nn.silu(c_ctx)[None] @ w_mod[l] + b_mod[l]
        mod_lat = jax.nn.silu(c) @ w_mod[l] + b_mod[l]
        xp, (k_l, v_l, s_l, h_l) = trunk_layer(xp, mod_ctx, lp, lb_all[l], None, None)
        xs = trunk_layer(xs, mod_lat, lp, lb_all[l], pos,
                         (cache_k[:, l], cache_v[:, l], state_ssm[:, l], state_hgrn[:, l]))
        ks_l.append(k_l)
        vs_l.append(v_l)
        ss_l.append(s_l)
        hs_l.append(h_l)
    new_cache_k = jnp.stack(ks_l, axis=1)
    new_cache_v = jnp.stack(vs_l, axis=1)
    new_state_ssm = jnp.stack(ss_l, axis=1)
    new_state_hgrn = jnp.stack(hs_l, axis=1)
    return (xp, xs, new_cache_k, new_cache_v, new_state_ssm, new_state_hgrn)
```

```python
import numpy as np
import concourse.bass as bass
import concourse.mybir as mybir
from concourse.bass_utils import run_bass_kernel_spmd

F32 = mybir.dt.float32
BF16 = mybir.dt.bfloat16
AF = mybir.ActivationFunctionType
ALU = mybir.AluOpType
AX = mybir.AxisListType

D = 4096
DEPTH = 2
NCORE = 8
NSEG = 6
SEGL = 256
NTOK = NSEG * SEGL
PT = 512
EPS = 1e-6
AQ0, AK0, AV0, AG0 = 0, 2048, 2560, 3072
BX0, BZ0, BB0, BC0, BDT0 = 5120, 7168, 9216, 9728, 10240
CQ0, CF0, CI0, CG0, MG0 = 10304, 12352, 16448, 18496, 20544
N_IN = 32832
NEG = -30000.0

C_ID, C_ONE, C_MF, C_MB, C_SLF, C_SUB, C_RM, C_BDF, C_BDB = [i * 128 for i in range(9)]
C_RST = 9 * 128
C_END = C_RST + 512


class StopBuild(Exception):
    pass


class Sched:
    def __init__(self, nc):
        self.nc = nc
        self.E = {'pe': nc.tensor, 'dve': nc.vector, 'act': nc.scalar, 'pool': nc.gpsimd, 'sp': nc.sync}
        self.sems = {}
        for e in self.E:
            self.sems[e] = nc.alloc_semaphore('s_' + e)
        self.cnt = {e: 0 for e in self.E}
        self.seen = {e: {} for e in self.E}
        self.lastw = {}
        self.rd = {}
        self.dq = {}
        for q, n in (('sp', 16), ('pool', 8)):
            ids = []
            for i in range(n):
                sid = 'd_%s%d' % (q, i)
                self.sems[sid] = nc.alloc_semaphore(sid)
                ids.append(sid)
            self.dq[q] = dict(ids=ids, vals=[0] * n, idx=0)
        self.nops = 0
        self.maxops = None
        self.noself = False

    def _wait(self, e, need):
        for sid, val in need.items():
            if self.seen[e].get(sid, 0) < val:
                self.E[e].wait_ge(self.sems[sid], val)
                self.seen[e][sid] = val

    def _deps(self, e, reads, writes):
        need = {}

        def add(sid, val):
            if sid == 'pe' and e == 'pe':
                return
            if self.noself and sid == e:
                return
            if need.get(sid, 0) < val:
                need[sid] = val
        for k in reads:
            if k in self.lastw:
                add(*self.lastw[k])
            if isinstance(k, tuple) and k[0] == 'ps':
                for sid, val in self.rd.get(k, {}).items():
                    if sid != e:
                        add(sid, val)
        for k in writes:
            if k in self.lastw:
                add(*self.lastw[k])
            for sid, val in self.rd.get(k, {}).items():
                add(sid, val)
        return need

    def _commit(self, tok, reads, writes):
        sid, val = tok
        for k in reads:
            d = self.rd.setdefault(k, {})
            if d.get(sid, 0) < val:
                d[sid] = val
        for k in writes:
            self.lastw[k] = tok
            self.rd[k] = {}

    def op(self, e, fn, reads=(), writes=()):
        if self.maxops is not None and self.nops >= self.maxops:
            raise StopBuild()
        self._wait(e, self._deps(e, reads, writes))
        ins = fn(self.E[e])
        self.cnt[e] += 1
        ins.then_inc(self.sems[e], 1)
        self._commit((e, self.cnt[e]), reads, writes)
        self.nops += 1

    def dma(self, q, out, in_, reads=(), writes=()):
        if self.maxops is not None and self.nops >= self.maxops:
            raise StopBuild()
        need = self._deps(q, reads, writes)
        d = self.dq[q]
        i = d['idx']
        d['idx'] = (i + 1) % len(d['ids'])
        sid = d['ids'][i]
        if d['vals'][i] > 0:
            need[sid] = max(need.get(sid, 0), d['vals'][i])
        self._wait(q, need)
        self.E[q].dma_start(out=out, in_=in_, allow_slow_non_contiguous=True).then_inc(self.sems[sid], 16)
        d['vals'][i] += 16
        self._commit((sid, d['vals'][i]), reads, writes)
        self.nops += 1

    def barrier(self):
        tgt = {}
        for e in self.E:
            if self.cnt[e] > 0:
                tgt[e] = self.cnt[e]
        for q in self.dq:
            d = self.dq[q]
            for sid, v in zip(d['ids'], d['vals']):
                if v > 0:
                    tgt[sid] = v
        for e in self.E:
            self._wait(e, dict(tgt))
        self.lastw = {}
        self.rd = {}

    def finish(self):
        self.barrier()


def build_program(cfg):
    nlayers = cfg.get('nlayers', DEPTH)
    LW = cfg.get('wlayers', DEPTH)
    branches = cfg.get('branches', (0, 1, 2))
    nc = bass.Bass("TRN2", target_bir_lowering=False)
    S = Sched(nc)
    S.maxops = cfg.get('maxops')
    S.noself = cfg.get('noself', False)

    def din(name, shape, dt=F32):
        return nc.dram_tensor(name, list(shape), dt, kind="ExternalInput").ap()

    def dout(name, shape, dt=F32):
        return nc.dram_tensor(name, list(shape), dt, kind="ExternalOutput").ap()

    def dscr(name, shape, dt=F32):
        return nc.dram_tensor(name, list(shape), dt, kind="Internal").ap()

    x_in = din("x", [NTOK, D])
    c2 = din("c2", [2, D])
    cache_k = din("cache_k", [DEPTH, 4, 128, 512])
    cache_v = din("cache_v", [DEPTH, 512, 512])
    st_ssm = din("st_ssm", [DEPTH, 2, 4, 128, 512])
    st_hg = din("st_hg", [DEPTH, 2, 16, 128, 128])
    cst = din("cst", [128, C_END])
    ropec = din("ropec", [128, NTOK])
    ropes = din("ropes", [128, NTOK])
    abias_g = din("abias_g", [128, 4 * 12])
    abias_s = din("abias_s", [128, 2 * 4])
    flags = din("flags", [128, 4])
    ln_w = din("ln_w", [DEPTH, D])
    TINY = cfg.get('tiny', False)
    w_mod = din("w_mod", [LW, D, 3 * D] if not TINY else [1, 128, 3 * D])
    b_mod = din("b_mod", [DEPTH, 3 * D])
    w_in = din("w_in", [LW, D, N_IN] if not TINY else [1, 128, N_IN])
    q_norm_w = din("q_norm_w", [DEPTH, 128])
    k_norm_w = din("k_norm_w", [DEPTH, 128])
    conv_w = din("conv_w", [DEPTH, 3, 3072])
    conv_b = din("conv_b", [DEPTH, 3072])
    dt_bias = din("dt_bias", [DEPTH, 64])
    a_log = din("a_log", [DEPTH, 64])
    d_skip = din("d_skip", [DEPTH, 32])
    ssm_norm_w = din("ssm_norm_w", [DEPTH, 2048])
    hgrn_lb = din("hgrn_lb", [DEPTH, 2, 2048])
    hgrn_norm_w = din("hgrn_norm_w", [DEPTH, 2048])
    w_bout = din("w_bout", [LW, 3, 2048, D] if not TINY else [1, 3, 128, D])
    w_o = din("w_o", [LW, D, D] if not TINY else [1, 128, D])

    y_out = dout("y", [NTOK, D])
    ck_out = dout("ck", [NSEG, DEPTH, 4, 128, SEGL])
    cv_out = dout("cv", [NSEG, DEPTH, SEGL, 512])
    ss_out = dout("ss", [NSEG, DEPTH, 2, 4, 128, 512])
    sh_out = dout("sh", [NSEG, DEPTH, 2, 16, 128, 128])

    kT_scr = dscr("kT_scr", [4, 128, 2048], BF16)
    v_scr = dscr("v_scr", [2048, 512], BF16)
    sbp_ssm = dscr("sbp_ssm", [4, 128, 512])
    sbp_hg = dscr("sbp_hg", [16, 128, 128])
    sfc_ssm = dscr("sfc_ssm", [4, 128, 512])
    sfc_hg = dscr("sfc_hg", [16, 128, 128])
    x1_scr = dscr("x1_scr", [NTOK, D])

    def sb(name, shape, dt=F32):
        return nc.alloc_sbuf_tensor(name, list(shape), dt)

    cstf = sb("cstf", [128, C_END])
    cstb = sb("cstb", [128, C_END], BF16)
    cc = sb("cc", [128, 8])
    ropec_t = sb("ropec_t", [128, NTOK], BF16)
    ropes_t = sb("ropes_t", [128, NTOK], BF16)
    abg = sb("abg", [128, 48])
    abs_ = sb("abs_", [128, 8])
    flg = sb("flg", [128, 4])
    cT = sb("cT", [128, 32, 2])
    scT = sb("scT", [128, 32, 2], BF16)
    lnwT = sb("lnwT", [128, 32])
    bmodT = sb("bmodT", [128, 96, 2])
    modT = sb("modT", [128, 96, 2])
    g1 = sb("g1", [128, 32, 2])
    qw = sb("qw", [128, 1])
    kw = sb("kw", [128, 1])
    cw = sb("cw", [128, 24, 3])
    cwl = sb("cwl", [128, 24, 2])
    cb = sb("cb", [128, 24])
    dtb_b = sb("dtb_b", [128, 64])
    a_b = sb("a_b", [128, 64])
    dsk_b = sb("dsk_b", [128, 32])
    hnw = sb("hnw", [128, 16])
    lbraw = sb("lbraw", [128, 2, 2, 16])
    lbe = sb("lbe", [128, 2, 2, 16])
    lbs = sb("lbs", [128, 2, 16])
    lbT = sb("lbT", [128, 2, 2, 16])
    omlT = sb("omlT", [128, 2, 2, 16])
    nomlT = sb("nomlT", [128, 2, 2, 16])
    hT = sb("hT", [128, 32, PT], BF16)
    hTe = sb("hTe", [128, 32, 128], BF16)
    mgT = sb("mgT", [128, 32, PT], BF16)
    oT = sb("oT", [128, 16, PT], BF16)
    NSLOT = 4
    ring = [sb("ring%d" % i, [128, 32, 128], BF16) for i in range(NSLOT)]
    work = sb("work", [128, 15360])
    small = sb("small", [128, 64])

    ps_all = nc.alloc_psum_tensor("ps_all", [128, 8, 512], F32) if hasattr(nc, 'alloc_psum_tensor') else None
    assert ps_all is not None

    def PS(b):
        return ps_all[:, b, :]

    def PSB(b):
        return ps_all[:, b, :].bitcast(BF16)

    rot = {'i': 0}

    def tbank():
        b = rot['i'] % 4
        rot['i'] += 1
        return b

    def cf(col, n=128):
        return cstf[:, col:col + n]

    def cbf(col, n=128):
        return cstb[:, col:col + n]

    xb = [mgT[:, 0:16, :].rearrange("p a b -> p (a b)").bitcast(F32),
          mgT[:, 16:32, :].rearrange("p a b -> p (a b)").bitcast(F32)]
    xn = oT[:, 0:8, :].rearrange("p a b -> p (a b)")

    class Arena:
        def __init__(self):
            self.off = 0

        def reset(self):
            self.off = 0

        def get(self, words_f32, dt=F32, shape=None):
            a = work[:, self.off:self.off + words_f32]
            self.off += words_f32
            assert self.off <= 15360, self.off
            if dt == BF16:
                a = a.bitcast(BF16)
            if shape is not None:
                if len(shape) == 3:
                    a = a.rearrange("p (a b) -> p a b", b=shape[2])
                elif len(shape) == 4:
                    a = a.rearrange("p (a b c) -> p a b c", b=shape[2], c=shape[3])
            return a
    AR = Arena()

    class WStream:
        def __init__(self):
            self.items = []

        def add(self, wsrc, nk, fn, ncol=128):
            self.items.append((wsrc, nk, fn, ncol))

        def run(self, depth=2):
            items = self.items
            self.items = []
            n = len(items)
            issued = 0
            st = WStream.state

            def issue(j):
                wsrc, nk, fn, ncol = items[j]
                if wsrc is None:
                    return None
                slot = st['n'] % NSLOT
                st['n'] += 1
                pieces = wsrc if isinstance(wsrc, list) else [(wsrc, 0)]
                for ap, c0 in pieces:
                    if TINY:
                        continue
                    w = ap.shape[1]
                    S.dma('pool', ring[slot][:, 0:nk, c0:c0 + w],
                          ap.rearrange("(k p) c -> p k c", p=128),
                          reads=(), writes=[('ring', slot)])
                return slot
            slots = {}
            for i in range(n):
                while issued < n and issued <= i + depth:
                    pending = [j for j in range(i, issued) if slots.get(j) is not None]
                    if items[issued][0] is not None and len(pending) >= NSLOT:
                        break
                    slots[issued] = issue(issued)
                    issued += 1
                sl = slots[i]
                items[i][2](None if sl is None else (ring[sl], ('ring', sl)))
    WStream.state = {'n': 0}
    WS = WStream()

    def fm_mm(slot, nk, src, srckey, cols, bank, ncolw=128):
        rt, rk = slot
        c0, n = cols

        def f(e):
            ins = None
            for k in range(nk):
                ins = e.matmul(PS(bank)[0:ncolw, 0:n], rt[:, k, 0:ncolw], src[:, k, c0:c0 + n],
                               start=(k == 0), stop=(k == nk - 1))
            return ins
        keys = [rk] + ([('mg', 0), ('mg', 1)] if srckey is None else [srckey])
        S.op('pe', f, reads=keys, writes=[('ps', bank)])

    def tm_mm(slot, nk, src, srckey, tiles, bank, ncolw=128):
        rt, rk = slot

        def f(e):
            ins = None
            for j, t0 in enumerate(tiles):
                for k in range(nk):
                    ins = e.matmul(PS(bank)[:, j * ncolw:(j + 1) * ncolw], src[:, k, t0:t0 + 128],
                                   rt[:, k, 0:ncolw], start=(k == 0), stop=(k == nk - 1))
            return ins
        S.op('pe', f, reads=[rk, srckey], writes=[('ps', bank)])

    def rsqrt_(dst, src, scale, rk, wk):
        S.op('act', lambda e: e.activation(out=dst, in_=src, func=AF.Ln, bias=cc[:, 0:1], scale=scale),
             reads=rk, writes=wk)
        S.op('act', lambda e: e.activation(out=dst, in_=dst, func=AF.Exp, scale=-0.5), reads=wk, writes=wk)

    stg = sb("stg", [128, 128])

    def loadT(dst, src1d, n, wkey):
        S.dma('sp', stg[0:n, :], src1d.rearrange("(k p) -> k p", p=128), writes=['stg'])
        b = tbank()
        S.op('pe', lambda e: e.transpose(PS(b)[:, 0:n], stg[0:n, :], cstf[0:n, C_ID:C_ID + n]),
             reads=['stg', 'cstf'], writes=[('ps', b)])
        S.op('dve', lambda e: e.tensor_copy(out=dst, in_=PS(b)[:, 0:n]), reads=[('ps', b)], writes=[wkey])

    class Stop(Exception):
        pass

    def stage(name):
        if cfg.get('stop') == name:
            raise Stop()

    S.dma('sp', cstf[:, :], cst[:, :], writes=['cstf'])
    S.op('dve', lambda e: e.tensor_copy(out=cstb[:, :], in_=cstf[:, :]), reads=['cstf'], writes=['cstb'])
    S.op('dve', lambda e: e.memset(cc[:, 0:1], EPS), writes=['cc'])
    S.op('dve', lambda e: e.memset(cc[:, 1:2], 1.0), writes=['cc'])
    S.op('dve', lambda e: e.memset(cc[:, 2:3], 0.0), writes=['cc'])
    tmpr = work[:, 0:NTOK]
    S.dma('sp', tmpr, ropec[:, :], writes=['tmpr'])
    S.op('dve', lambda e: e.tensor_copy(out=ropec_t[:, :], in_=tmpr), reads=['tmpr'], writes=['rope'])
    tmpr2 = work[:, NTOK:2 * NTOK]
    S.dma('sp', tmpr2, ropes[:, :], writes=['tmpr2'])
    S.op('dve', lambda e: e.tensor_copy(out=ropes_t[:, :], in_=tmpr2), reads=['tmpr2'], writes=['rope'])
    S.dma('sp', abg[:, :], abias_g[:, :], writes=['abg'])
    S.dma('sp', abs_[:, :], abias_s[:, :], writes=['abs'])
    S.dma('sp', flg[:, :], flags[:, :], writes=['flg'])
    for l_ in range(2):
        for d_ in range(2):
            loadT(lbraw[:, l_, d_, :], hgrn_lb[l_, d_], 16, 'lbraw')
    S.op('act', lambda e: e.activation(out=lbe[:, :, :, :], in_=lbraw[:, :, :, :], func=AF.Exp),
         reads=['lbraw'], writes=['lbe'])
    S.op('dve', lambda e: e.tensor_tensor(out=lbs[:, :, :], in0=lbe[:, 0, :, :], in1=lbe[:, 1, :, :], op=ALU.add),
         reads=['lbe'], writes=['lbs'])
    S.op('dve', lambda e: e.reciprocal(out=lbs[:, :, :], in_=lbs[:, :, :]), reads=['lbs'], writes=['lbs'])
    S.op('dve', lambda e: e.tensor_tensor(out=lbe[:, 0, :, :], in0=lbe[:, 0, :, :], in1=lbs[:, :, :], op=ALU.mult),
         reads=['lbe', 'lbs'], writes=['lbe'])
    S.op('dve', lambda e: e.tensor_tensor(out=lbe[:, 1, :, :], in0=lbe[:, 1, :, :], in1=lbs[:, :, :], op=ALU.mult),
         reads=['lbe', 'lbs'], writes=['lbe'])
    S.op('dve', lambda e: e.tensor_tensor(out=lbT[:, 0, :, :], in0=lbe[:, 0, :, :], in1=lbe[:, 0, :, :], op=ALU.subtract),
         reads=['lbe'], writes=['lbT'])
    S.op('dve', lambda e: e.tensor_tensor(out=lbT[:, 1, :, :], in0=lbe[:, 0, :, :], in1=lbe[:, 1, :, :], op=ALU.add),
         reads=['lbe'], writes=['lbT'])
    S.op('dve', lambda e: e.tensor_tensor(out=lbT[:, 1, :, :], in0=lbT[:, 1, :, :], in1=lbe[:, 0, :, :], op=ALU.subtract),
         reads=['lbe', 'lbT'], writes=['lbT'])
    S.op('dve', lambda e: e.tensor_scalar(out=omlT[:, :, :, :], in0=lbT[:, :, :, :], scalar1=-1.0, scalar2=1.0,
                                          op0=ALU.mult, op1=ALU.add), reads=['lbT'], writes=['omlT'])
    S.op('dve', lambda e: e.tensor_scalar(out=nomlT[:, :, :, :], in0=lbT[:, :, :, :], scalar1=-1.0, scalar2=None,
                                          op0=ALU.add), reads=['lbT'], writes=['nomlT'])
    if TINY:
        for i_ in range(NSLOT):
            S.op('dve', lambda e, i_=i_: e.memset(ring[i_][:, :, :], 0.0), writes=[('ring', i_)])
    S.barrier()
    def layer(L):
        xsrc = x_in if L == 0 else x1_scr
        xdst = y_out if L == nlayers - 1 else x1_scr
        XK = ('xres',)

        for r in range(2):
            loadT(cT[:, :, r], c2[r], 32, 'cT')
        loadT(lnwT[:, :], ln_w[L], 32, 'lnwT')
        for r in range(2):
            loadT(bmodT[:, :, r], b_mod[L], 96, 'bmodT')
        S.dma('sp', qw[:, :], q_norm_w[L].rearrange("(p o) -> p o", o=1), writes=['qw'])
        S.dma('sp', kw[:, :], k_norm_w[L].rearrange("(p o) -> p o", o=1), writes=['kw'])
        for t in range(3):
            loadT(cw[:, :, t], conv_w[L, t], 24, 'cw')
        loadT(cb[:, :], conv_b[L], 24, 'cb')
        S.dma('sp', dtb_b[:, :], dt_bias[L].partition_broadcast(128), writes=['dtb'])
        S.dma('sp', a_b[:, :], a_log[L].partition_broadcast(128), writes=['a_b'])
        S.dma('sp', dsk_b[:, :], d_skip[L].partition_broadcast(128), writes=['dsk'])
        loadT(hnw[:, :], hgrn_norm_w[L], 16, 'hnw')
        S.op('act', lambda e: e.activation(out=a_b[:, :], in_=a_b[:, :], func=AF.Exp), reads=['a_b'], writes=['a_b'])
        S.op('dve', lambda e: e.tensor_scalar(out=a_b[:, :], in0=a_b[:, :], scalar1=-1.0, scalar2=None, op0=ALU.mult),
             reads=['a_b'], writes=['a_b'])
        S.op('dve', lambda e: e.tensor_scalar(out=cwl[:, :, 0], in0=cw[:, :, 0], scalar1=flg[:, 0:1], scalar2=None,
                                              op0=ALU.mult), reads=['cw', 'flg'], writes=['cwl'])
        S.op('dve', lambda e: e.tensor_scalar(out=cwl[:, :, 1], in0=cw[:, :, 2], scalar1=flg[:, 0:1], scalar2=None,
                                              op0=ALU.mult), reads=['cw', 'flg'], writes=['cwl'])
        S.op('act', lambda e: e.activation(out=scT[:, :, :], in_=cT[:, :, :], func=AF.Silu), reads=['cT'], writes=['scT'])

        stage('params')
        MB = 7
        for cbk in range(96 if not TINY else 0):
            def fn(slot, cbk=cbk):
                rt, rk = slot

                def f(e):
                    ins = None
                    for k in range(32):
                        ins = e.matmul(PS(MB)[:, cbk * 2:cbk * 2 + 2], rt[:, k, :], scT[:, k, :],
                                       start=(k == 0), stop=(k == 31))
                    return ins
                S.op('pe', f, reads=[rk, 'scT'], writes=[('ps', MB)])
            WS.add(w_mod[L][:, cbk * 128:(cbk + 1) * 128], 32, fn)
        WS.run()
        S.op('dve', lambda e: e.tensor_tensor(out=modT[:, :, :], in0=PS(MB)[:, 0:192].rearrange("p (a b) -> p a b", b=2),
                                              in1=bmodT[:, :, :], op=ALU.add),
             reads=[('ps', MB), 'bmodT'], writes=['modT'])
        S.op('dve', lambda e: e.scalar_tensor_tensor(out=g1[:, :, :], in0=modT[:, 32:64, :], scalar=1.0,
                                                     in1=lnwT[:, :].unsqueeze(2).broadcast_to([128, 32, 2]),
                                                     op0=ALU.add, op1=ALU.mult),
             reads=['modT', 'lnwT'], writes=['g1'])

        def norm_tile(i, row0, dst, dcol, mrow):
            xt = xb[i % 2]
            xk = ('mg', i % 2)
            S.dma('sp', xt, xsrc[row0:row0 + 128, :], reads=[XK], writes=[xk])
            ss = small[:, 0:1]
            S.op('act', lambda e: e.activation(out=xn, in_=xt, func=AF.Square, accum_out=ss),
                 reads=[xk], writes=['oT', 'ss'])
            rsqrt_(ss, ss, 1.0 / D, ['ss'], ['ss'])
            S.op('dve', lambda e: e.tensor_scalar(out=xn, in0=xt, scalar1=ss, scalar2=None, op0=ALU.mult),
                 reads=[xk, 'ss'], writes=['oT'])
            for q in range(4):
                b = tbank()

                def f(e, q=q, b=b):
                    ins = None
                    for j in range(8):
                        kc = q * 8 + j
                        ins = e.transpose(PSB(b)[:, j * 128:(j + 1) * 128], xn[:, kc * 128:(kc + 1) * 128], cbf(C_ID))
                    return ins
                S.op('pe', f, reads=['oT', 'cstb'], writes=[('ps', b)])
                for j in range(8):
                    kc = q * 8 + j
                    S.op('act', lambda e, j=j, kc=kc, b=b: e.activation(
                        out=dst[:, kc, dcol:dcol + 128], in_=PSB(b)[:, j * 128:(j + 1) * 128], func=AF.Identity,
                        scale=g1[:, kc, mrow:mrow + 1], bias=modT[:, kc, mrow:mrow + 1]),
                        reads=[('ps', b), 'g1', 'modT'], writes=[('hT', id(dst))])

        def norm_phase(tiles, ext_tile, mrow):
            for i, t in enumerate(tiles):
                norm_tile(i, t * 128, hT, i * 128, mrow)
            if ext_tile is not None:
                norm_tile(len(tiles), ext_tile * 128, hTe, 0, mrow)
        HK = ('hT', id(hT))
        HEK = ('hT', id(hTe))

        def kv_step(tiles, segs):
            AR.reset()
            raw = AR.get(512)
            sq = AR.get(256, BF16)
            rstd = AR.get(512)
            knf = AR.get(512)
            knb = AR.get(256, BF16)
            t1 = AR.get(512)
            krp = AR.get(256, BF16)
            vst = AR.get(512)
            vbf = AR.get(256, BF16)
            tok0 = tiles[0] * 128
            for g in range(4):
                def fk(slot, g=g):
                    if cfg.get('kvbank'):
                        rot['i'] = cfg['kvbank']
                    b = tbank()
                    fm_mm(slot, 32, hT, HK, (0, 512), b)
                    S.op('act', lambda e: e.activation(out=sq, in_=PS(b), func=AF.Square), reads=[('ps', b)], writes=['kv_sq'])
                    S.op('dve', lambda e: e.tensor_copy(out=raw, in_=PS(b)), reads=[('ps', b)], writes=['kv_raw'])
                    b2 = tbank()
                    S.op('pe', lambda e: e.matmul(PS(b2), cbf(C_ONE), sq, start=True, stop=True),
                         reads=['kv_sq', 'cstb'], writes=[('ps', b2)])
                    rsqrt_(rstd, PS(b2), 1.0 / 128, [('ps', b2)], ['kv_rstd'])
                    S.op('dve', lambda e: e.scalar_tensor_tensor(out=knf, in0=raw, scalar=kw[:, 0:1], in1=rstd,
                                                                 op0=ALU.mult, op1=ALU.mult),
                         reads=['kv_raw', 'kw', 'kv_rstd'], writes=['kv_knf'])
                    if cfg.get('kvl', 9) < 2:
                        return
                    for si, sg in enumerate(segs):
                        S.dma('sp', ck_out[sg, L, g, :, :], knf[:, si * 256:(si + 1) * 256], reads=['kv_knf'], writes=[('ck', sg, g)])
                    if cfg.get('kvl', 9) < 3:
                        return
                    S.op('act', lambda e: e.activation(out=knb, in_=knf, func=AF.Copy), reads=['kv_knf'], writes=['kv_knb'])
                    b3 = tbank()
                    S.op('pe', lambda e: e.matmul(PS(b3), cbf(C_RM), knb, start=True, stop=True),
                         reads=['kv_knb', 'cstb'], writes=[('ps', b3)])
                    S.op('dve', lambda e: e.tensor_tensor(out=t1, in0=knb, in1=ropec_t[:, tok0:tok0 + 512], op=ALU.mult),
                         reads=['kv_knb', 'rope'], writes=['kv_t1'])
                    S.op('dve', lambda e: e.tensor_tensor(out=rstd, in0=PS(b3), in1=ropes_t[:, tok0:tok0 + 512], op=ALU.mult),
                         reads=[('ps', b3), 'rope'], writes=['kv_rstd'])
                    S.op('dve', lambda e: e.tensor_tensor(out=krp, in0=t1, in1=rstd, op=ALU.add),
                         reads=['kv_t1', 'kv_rstd'], writes=['kv_krp'])
                    S.dma('sp', kT_scr[g, :, 512 + tok0:512 + tok0 + 512], krp, reads=['kv_krp'], writes=[('kTs', g)])
                WS.add(w_in[L][:, AK0 + g * 128:AK0 + (g + 1) * 128], 32, fk)

                def fv(slot, g=g):
                    if cfg.get('kvl', 9) < 4:
                        return
                    b = tbank()
                    tm_mm(slot, 32, hT, HK, [0, 128, 256, 384], b)
                    S.op('dve', lambda e: e.tensor_copy(out=vst, in_=PS(b)), reads=[('ps', b)], writes=['kv_vst'])
                    S.op('act', lambda e: e.activation(out=vbf, in_=PS(b), func=AF.Copy), reads=[('ps', b)], writes=['kv_vbf'])
                    for j in range(4):
                        sg = segs[j // 2]
                        S.dma('sp', cv_out[sg, L, (j % 2) * 128:(j % 2 + 1) * 128, g * 128:(g + 1) * 128],
                              vst[:, j * 128:(j + 1) * 128], reads=['kv_vst'], writes=[('cv', sg, g, j)])
                    S.dma('sp', v_scr[512 + tok0:512 + tok0 + 512, g * 128:(g + 1) * 128].rearrange("(t p) d -> p t d", p=128),
                          vbf.rearrange("p (t d) -> p t d", d=128), reads=['kv_vbf'], writes=[('vs', g)])
                WS.add(w_in[L][:, AV0 + g * 128:AV0 + (g + 1) * 128], 32, fv)
            WS.run()

        def cache_step():
            AR.reset()
            ckf = AR.get(512)
            ckb = AR.get(256, BF16)
            cvf = AR.get(2048)
            cvb = AR.get(1024, BF16)
            for g in range(4):
                S.dma('sp', ckf, cache_k[L, g, :, :], writes=['ckf'])
                S.op('dve', lambda e: e.tensor_copy(out=ckb, in_=ckf), reads=['ckf'], writes=['ckb'])
                S.dma('sp', kT_scr[g, :, 0:512], ckb, reads=['ckb'], writes=[('kTs', g)])
            S.dma('sp', cvf.rearrange("p (t c) -> p t c", c=512), cache_v[L].rearrange("(t p) c -> p t c", p=128), writes=['cvf'])
            S.op('dve', lambda e: e.tensor_copy(out=cvb, in_=cvf), reads=['cvf'], writes=['cvb'])
            S.dma('sp', v_scr[0:512, :].rearrange("(t p) c -> p t c", p=128), cvb.rearrange("p (t c) -> p t c", c=512),
                  reads=['cvb'], writes=[('vs', g) for g in range(4)])

        def branch_attn(P):
            AR.reset()
            raw = AR.get(512)
            sq = AR.get(256, BF16)
            rstd = AR.get(512)
            knb = AR.get(256, BF16)
            t1 = AR.get(512)
            qT = AR.get(1024, BF16, (128, 4, 512))
            gat = AR.get(1024, BF16, (128, 4, 512))
            nkt = P['nkt']
            kT = AR.get(nkt * 64, BF16)
            vv = AR.get(nkt * 64, BF16, (128, nkt, 128))
            PTr = [AR.get(256, BF16) for _ in range(3)]
            rec = AR.get(512)
            tmp = AR.get(512)
            tok0 = P['tok0']
            krow0 = P['krow0']
            for g in range(4):
                for j in range(4):
                    hd = 4 * g + j

                    def fq(slot, j=j, hd=hd):
                        b = tbank()
                        fm_mm(slot, 32, hT, HK, (0, 512), b)
                        S.op('act', lambda e: e.activation(out=sq, in_=PS(b), func=AF.Square), reads=[('ps', b)], writes=['a_sq'])
                        S.op('dve', lambda e: e.tensor_copy(out=raw, in_=PS(b)), reads=[('ps', b)], writes=['a_raw'])
                        b2 = tbank()
                        S.op('pe', lambda e: e.matmul(PS(b2), cbf(C_ONE), sq, start=True, stop=True),
                             reads=['a_sq', 'cstb'], writes=[('ps', b2)])
                        rsqrt_(rstd, PS(b2), 1.0 / 128, [('ps', b2)], ['a_rstd'])
                        S.op('dve', lambda e: e.scalar_tensor_tensor(out=knb, in0=raw, scalar=qw[:, 0:1], in1=rstd,
                                                                     op0=ALU.mult, op1=ALU.mult),
                             reads=['a_raw', 'qw', 'a_rstd'], writes=['a_knb'])
                        b3 = tbank()
                        S.op('pe', lambda e: e.matmul(PS(b3), cbf(C_RM), knb, start=True, stop=True),
                             reads=['a_knb', 'cstb'], writes=[('ps', b3)])
                        S.op('dve', lambda e: e.tensor_tensor(out=t1, in0=knb, in1=ropec_t[:, tok0:tok0 + 512], op=ALU.mult),
                             reads=['a_knb', 'rope'], writes=['a_t1'])
                        S.op('dve', lambda e: e.tensor_tensor(out=rstd, in0=PS(b3), in1=ropes_t[:, tok0:tok0 + 512], op=ALU.mult),
                             reads=[('ps', b3), 'rope'], writes=['a_rstd'])
                        S.op('dve', lambda e: e.tensor_tensor(out=qT[:, j, :], in0=t1, in1=rstd, op=ALU.add),
                             reads=['a_t1', 'a_rstd'], writes=[('a_qT', j)])
                    WS.add(w_in[L][:, AQ0 + hd * 128:AQ0 + (hd + 1) * 128], 32, fq)

                    def fg(slot, j=j, hd=hd, g=g):
                        b = tbank()
                        fm_mm(slot, 32, hT, HK, (0, 512), b)
                        S.op('act', lambda e: e.activation(out=gat[:, j, :], in_=PS(b), func=AF.Silu),
                             reads=[('ps', b)], writes=[('a_gat', j)])
                        if j == 3:
                            attn_group(g)
                    WS.add(w_in[L][:, AG0 + hd * 128:AG0 + (hd + 1) * 128], 32, fg)

            def attn_group(g):
                S.dma('sp', kT, kT_scr[g, :, krow0:krow0 + nkt * 128], reads=[('kTs', g)], writes=['a_kT'])
                S.dma('sp', vv, v_scr[krow0:krow0 + nkt * 128, g * 128:(g + 1) * 128].rearrange("(t p) d -> p t d", p=128),
                      reads=[('vs', g)], writes=['a_vv'])
                for j in range(4):
                    hd = 4 * g + j
                    ob, sbk = (4, 5) if (hd % 2 == 0) else (6, 7)
                    scb = {}

                    def emit_sc(kt, j=j):
                        b = tbank()
                        scb[kt] = b
                        S.op('pe', lambda e: e.matmul(PS(b), kT[:, kt * 128:(kt + 1) * 128], qT[:, j, :], start=True, stop=True),
                             reads=['a_kT', ('a_qT', j)], writes=[('ps', b)])

                    def emit_pv(kt):
                        b = scb[kt]
                        pt = PTr[kt % 3]
                        pk = ('a_pt', kt % 3)
                        for s in range(2):
                            bias = P['bias'](kt, s)
                            S.op('act', lambda e, s=s, bias=bias: e.activation(
                                out=pt[:, s * 256:(s + 1) * 256], in_=PS(b)[:, s * 256:(s + 1) * 256], func=AF.Exp,
                                bias=bias, scale=128 ** -0.5), reads=[('ps', b), 'abg', 'abs'], writes=[pk])
                        S.op('pe', lambda e: e.matmul(PS(ob), vv[:, kt, :], pt, start=(kt == 0), stop=(kt == nkt - 1)),
                             reads=['a_vv', pk], writes=[('ps', ob)])
                        S.op('pe', lambda e: e.matmul(PS(sbk), cbf(C_ONE), pt, start=(kt == 0), stop=(kt == nkt - 1)),
                             reads=['cstb', pk], writes=[('ps', sbk)])
                    emit_sc(0)
                    for kt in range(nkt):
                        if kt + 1 < nkt:
                            emit_sc(kt + 1)
                        emit_pv(kt)
                    S.op('dve', lambda e: e.reciprocal(out=rec, in_=PS(sbk)), reads=[('ps', sbk)], writes=['a_rec'])
                    S.op('dve', lambda e: e.tensor_tensor(out=tmp, in0=PS(ob), in1=rec, op=ALU.mult),
                         reads=[('ps', ob), 'a_rec'], writes=['a_tmp'])
                    S.op('dve', lambda e, j=j, hd=hd: e.tensor_tensor(out=oT[:, hd, :], in0=tmp, in1=gat[:, j, :], op=ALU.mult),
                         reads=['a_tmp', ('a_gat', j)], writes=['oT'])
            WS.run()

        def init_state(Sst, key, kind, src, link):
            if kind == 'zero':
                S.op('dve', lambda e: e.memset(Sst, 0.0), writes=[key])
            else:
                S.dma('sp', Sst, src, reads=[('scr', str(src.tensor.name) if hasattr(src, 'tensor') else 'x')], writes=[key])
                if link is not None:
                    S.op('dve', lambda e: e.tensor_scalar(out=Sst, in0=Sst, scalar1=link, scalar2=None, op0=ALU.mult),
                         reads=[key, 'flg'], writes=[key])

        def branch_ssd(P, mode):
            pre = (mode == 'pre')
            pi = P['pi']
            ext = P['ext'] is not None
            ea = P['ext_after']
            ecol = 0 if ea else 127
            linkmid = flg[:, 0:1] if P['group'] else flg[:, 1:2]
            AR.reset()
            raw = AR.get(512)
            rawe = AR.get(2)
            acc = AR.get(512)
            xcT = AR.get(1024, BF16, (128, 4, 512))
            BcT = AR.get(256, BF16)
            CcT = AR.get(256, BF16)
            xtm = AR.get(1024, BF16, (128, 4, 512))
            btm = AR.get(256, BF16, (128, 4, 128))
            zs = AR.get(1024, BF16, (128, 4, 512))
            dtt = AR.get(64, F32, (128, 4, 16))
            dta = AR.get(64, F32, (128, 4, 16))
            dtbg = AR.get(16)
            ag = AR.get(16)
            xpf = AR.get(1024, BF16, (128, 4, 512))
            xpb = AR.get(256, BF16)
            xpp = AR.get(256, BF16)
            Sst = [AR.get(512), AR.get(512)]
            Sfbf = AR.get(1024, BF16, (128, 4, 512))
            Sbbf = AR.get(256, BF16)
            Br = [AR.get(512, BF16, (128, 8, 128)), AR.get(512, BF16, (128, 8, 128))]
            Ew = [AR.get(512, BF16, (128, 8, 128)), AR.get(512, BF16, (128, 8, 128))]
            Ea = [AR.get(512, BF16, (128, 8, 128)), AR.get(512, BF16, (128, 8, 128))]
            CBm = [AR.get(64, BF16), AR.get(64, BF16)]
            dec = AR.get(16)
            tS = acc
            t1 = AR.get(512)
            t2 = AR.get(512)
            ybf = AR.get(256, BF16)
            snw = AR.get(512)
            stS = [AR.get(512), AR.get(512)]
            sto = {'i': 0}

            def out_state(Ssrc, skey, dst):
                st = stS[sto['i'] % 2]
                sk = ('b_stS', sto['i'] % 2)
                sto['i'] += 1
                S.op('act', lambda e: e.activation(out=st, in_=Ssrc, func=AF.Copy), reads=[skey], writes=[sk])
                S.dma('sp', dst, st, reads=[sk], writes=[('sso', str(dst.offset))])

            def conv_block(slot, cbk, dst, dkey, need_ext=True):
                b = tbank()
                fm_mm(slot, 32, hT, HK, (0, 512), b)
                S.op('dve', lambda e: e.tensor_copy(out=raw, in_=PS(b)), reads=[('ps', b)], writes=['b_raw'])
                if ext and need_ext:
                    b2 = tbank()
                    fm_mm(slot, 32, hTe, HEK, (ecol, 1), b2)
                    S.op('dve', lambda e: e.tensor_copy(out=rawe[:, 0:1], in_=PS(b2)[:, 0:1]), reads=[('ps', b2)], writes=['b_rawe'])
                S.op('dve', lambda e: e.tensor_scalar(out=acc, in0=raw, scalar1=cw[:, cbk, 1:2], scalar2=cb[:, cbk:cbk + 1],
                                                      op0=ALU.mult, op1=ALU.add), reads=['b_raw', 'cw', 'cb'], writes=['b_acc'])
                r3 = raw.rearrange("p (s t) -> p s t", t=256)
                a3 = acc.rearrange("p (s t) -> p s t", t=256)
                S.op('dve', lambda e: e.scalar_tensor_tensor(out=a3[:, :, 1:256], in0=r3[:, :, 0:255], scalar=cw[:, cbk, 0:1],
                                                             in1=a3[:, :, 1:256], op0=ALU.mult, op1=ALU.add),
                     reads=['b_raw', 'b_acc', 'cw'], writes=['b_acc'])
                S.op('dve', lambda e: e.scalar_tensor_tensor(out=a3[:, :, 0:255], in0=r3[:, :, 1:256], scalar=cw[:, cbk, 2:3],
                                                             in1=a3[:, :, 0:255], op0=ALU.mult, op1=ALU.add),
                     reads=['b_raw', 'b_acc', 'cw'], writes=['b_acc'])
                if P['group']:
                    S.op('dve', lambda e: e.scalar_tensor_tensor(out=acc[:, 256:257], in0=raw[:, 255:256], scalar=cwl[:, cbk, 0:1],
                                                                 in1=acc[:, 256:257], op0=ALU.mult, op1=ALU.add),
                         reads=['b_raw', 'b_acc', 'cwl'], writes=['b_acc'])
                    S.op('dve', lambda e: e.scalar_tensor_tensor(out=acc[:, 255:256], in0=raw[:, 256:257], scalar=cwl[:, cbk, 1:2],
                                                                 in1=acc[:, 255:256], op0=ALU.mult, op1=ALU.add),
                         reads=['b_raw', 'b_acc', 'cwl'], writes=['b_acc'])
                if ext and need_ext:
                    if ea:
                        S.op('dve', lambda e: e.scalar_tensor_tensor(out=acc[:, 511:512], in0=rawe[:, 0:1], scalar=cwl[:, cbk, 1:2],
                                                                     in1=acc[:, 511:512], op0=ALU.mult, op1=ALU.add),
                             reads=['b_rawe', 'b_acc', 'cwl'], writes=['b_acc'])
                    else:
                        S.op('dve', lambda e: e.scalar_tensor_tensor(out=acc[:, 0:1], in0=rawe[:, 0:1], scalar=cwl[:, cbk, 0:1],
                                                                     in1=acc[:, 0:1], op0=ALU.mult, op1=ALU.add),
                             reads=['b_rawe', 'b_acc', 'cwl'], writes=['b_acc'])
                S.op('act', lambda e: e.activation(out=dst, in_=acc, func=AF.Silu), reads=['b_acc'], writes=[dkey])

            def state_update(di, t, g):
                fwd = (di == 0)
                dc = 0 if fwd else 8
                xp = xpf[:, t, :] if fwd else xpb
                xk = ('b_xpf', t) if fwd else 'b_xpb'
                b = tbank()

                def f(e):
                    e.matmul(PS(b)[:, 0:8], cf(C_SLF if fwd else C_SUB), dta[:, t, dc:dc + 8], start=True, stop=True)
                    return e.matmul(PS(b)[:, 8:16], cf(C_ONE), dta[:, t, dc:dc + 8], start=True, stop=True)
                S.op('pe', f, reads=['cstf', 'b_dta'], writes=[('ps', b)])
                S.op('act', lambda e: e.activation(out=dec, in_=PS(b)[:, 0:16], func=AF.Exp), reads=[('ps', b)], writes=['b_dec'])
                S.op('dve', lambda e: e.tensor_tensor(out=xpp.rearrange("p (h q) -> p h q", q=64),
                                                      in0=xp.rearrange("p (h q) -> p h q", q=64),
                                                      in1=dec[:, 0:8].unsqueeze(2).broadcast_to([128, 8, 64]), op=ALU.mult),
                     reads=[xk, 'b_dec'], writes=['b_xpp'])
                b2 = tbank()
                S.op('pe', lambda e: e.matmul(PS(b2), btm[:, t, :], xpp, start=True, stop=True),
                     reads=['b_btm', 'b_xpp'], writes=[('ps', b2)])
                sk = ('b_S', di)
                S.op('dve', lambda e: e.tensor_tensor(out=tS.rearrange("p (h q) -> p h q", q=64),
                                                      in0=Sst[di].rearrange("p (h q) -> p h q", q=64),
                                                      in1=dec[:, 8:16].unsqueeze(2).broadcast_to([128, 8, 64]), op=ALU.mult),
                     reads=[sk, 'b_dec'], writes=['b_tS'])
                S.op('dve', lambda e: e.tensor_tensor(out=Sst[di], in0=tS, in1=PS(b2), op=ALU.add),
                     reads=['b_tS', ('ps', b2)], writes=[sk])

            def ssd_group(g):
                gseg = P['segs']
                for t in range(4):
                    b = tbank()

                    def f(e, t=t, b=b):
                        ins = None
                        for j in range(4):
                            ins = e.transpose(PSB(b)[:, j * 128:(j + 1) * 128], xcT[:, j, t * 128:(t + 1) * 128], cbf(C_ID))
                        return e.transpose(PSB(b)[:, 512:640], BcT[:, t * 128:(t + 1) * 128], cbf(C_ID))
                    S.op('pe', f, reads=[('b_xcT', j) for j in range(4)] + ['b_BcT', 'cstb'], writes=[('ps', b)])
                    S.op('dve', lambda e, t=t, b=b: e.tensor_copy(out=xtm[:, t, :], in_=PSB(b)[:, 0:512]), reads=[('ps', b)], writes=['b_xtm'])
                    S.op('act', lambda e, t=t, b=b: e.activation(out=btm[:, t, :], in_=PSB(b)[:, 512:640], func=AF.Copy),
                         reads=[('ps', b)], writes=['b_btm'])
                if pre:
                    init_state(Sst[1], ('b_S', 1), 'dram', st_ssm[L, 1, g], None)
                else:
                    if pi == 0:
                        init_state(Sst[0], ('b_S', 0), 'dram', st_ssm[L, 0, g], None)
                        init_state(Sst[1], ('b_S', 1), 'dram', sbp_ssm[g], flg[:, 0:1])
                    elif pi == 1:
                        init_state(Sst[0], ('b_S', 0), 'dram', sfc_ssm[g], flg[:, 0:1])
                        init_state(Sst[1], ('b_S', 1), 'dram', st_ssm[L, 1, g], None)
                    else:
                        init_state(Sst[0], ('b_S', 0), 'zero', None, None)
                        init_state(Sst[1], ('b_S', 1), 'zero', None, None)
                if not pre:
                    for t in range(4):
                        S.op('dve', lambda e, t=t: e.tensor_tensor(out=xpf[:, t, :].rearrange("p (h q) -> p h q", q=64),
                                                                   in0=xtm[:, t, :].rearrange("p (h q) -> p h q", q=64),
                                                                   in1=dtt[:, t, 0:8].unsqueeze(2).broadcast_to([128, 8, 64]), op=ALU.mult),
                             reads=['b_xtm', 'b_dtt'], writes=[('b_xpf', t)])
                        if t == 2:
                            out_state(Sst[0], ('b_S', 0), ss_out[gseg[0], L, 0, g])
                            S.op('dve', lambda e: e.tensor_scalar(out=Sst[0], in0=Sst[0], scalar1=linkmid, scalar2=None, op0=ALU.mult),
                                 reads=[('b_S', 0), 'flg'], writes=[('b_S', 0)])
                        S.op('act', lambda e, t=t: e.activation(out=Sfbf[:, t, :], in_=Sst[0], func=AF.Copy),
                             reads=[('b_S', 0)], writes=[('b_Sfbf', t)])
                        state_update(0, t, g)
                    out_state(Sst[0], ('b_S', 0), ss_out[gseg[1], L, 0, g])
                    if pi == 0:
                        out_state(Sst[0], ('b_S', 0), sfc_ssm[g])
                for t in (3, 2, 1, 0):
                    if t == 1:
                        if not pre:
                            out_state(Sst[1], ('b_S', 1), ss_out[gseg[1], L, 1, g])
                        S.op('dve', lambda e: e.tensor_scalar(out=Sst[1], in0=Sst[1], scalar1=linkmid, scalar2=None, op0=ALU.mult),
                             reads=[('b_S', 1), 'flg'], writes=[('b_S', 1)])
                    S.op('dve', lambda e, t=t: e.tensor_tensor(out=xpb.rearrange("p (h q) -> p h q", q=64),
                                                               in0=xtm[:, t, :].rearrange("p (h q) -> p h q", q=64),
                                                               in1=dtt[:, t, 8:16].unsqueeze(2).broadcast_to([128, 8, 64]), op=ALU.mult),
                         reads=['b_xtm', 'b_dtt'], writes=['b_xpb'])
                    if not pre:
                        S.op('act', lambda e: e.activation(out=Sbbf, in_=Sst[1], func=AF.Copy), reads=[('b_S', 1)], writes=['b_Sbbf'])
                        tc_ = slice(t * 128, (t + 1) * 128)
                        bc_ = tbank()
                        S.op('pe', lambda e: e.matmul(PS(bc_)[:, 0:128], BcT[:, tc_], CcT[:, tc_], start=True, stop=True),
                             reads=['b_BcT', 'b_CcT'], writes=[('ps', bc_)])
                        S.op('dve', lambda e: e.tensor_tensor(out=CBm[0], in0=PS(bc_)[:, 0:128], in1=cbf(C_MF), op=ALU.mult),
                             reads=[('ps', bc_), 'cstb'], writes=[('b_CBm', 0)])
                        S.op('dve', lambda e: e.tensor_tensor(out=CBm[1], in0=PS(bc_)[:, 0:128], in1=cbf(C_MB), op=ALU.mult),
                             reads=[('ps', bc_), 'cstb'], writes=[('b_CBm', 1)])
                        for di in range(2):
                            dc = 0 if di == 0 else 8
                            Mm = C_MF if di == 0 else C_MB
                            Lm = C_SLF if di == 0 else C_SUB
                            S.op('dve', lambda e, di=di, dc=dc, Mm=Mm: e.tensor_tensor(
                                out=Br[di], in0=cbf(Mm).unsqueeze(1).broadcast_to([128, 8, 128]),
                                in1=dta[:, t, dc:dc + 8].unsqueeze(2).broadcast_to([128, 8, 128]), op=ALU.mult),
                                reads=['cstb', 'b_dta'], writes=[('b_Br', di)])
                            brf = Br[di].rearrange("p h l -> p (h l)")
                            ewf = Ew[di].rearrange("p h l -> p (h l)")
                            eaf = Ea[di].rearrange("p h l -> p (h l)")
                            for hf in range(2):
                                b0 = tbank()
                                S.op('pe', lambda e, b0=b0, hf=hf, Lm=Lm: e.matmul(PS(b0), cbf(Lm), brf[:, hf * 512:(hf + 1) * 512], start=True, stop=True),
                                     reads=['cstb', ('b_Br', di)], writes=[('ps', b0)])
                                S.op('act', lambda e, b0=b0, hf=hf: e.activation(out=ewf[:, hf * 512:(hf + 1) * 512], in_=PS(b0), func=AF.Exp),
                                     reads=[('ps', b0)], writes=[('b_Ew', di)])
                            for hf in range(2):
                                b0 = tbank()
                                S.op('pe', lambda e, b0=b0, hf=hf: e.matmul(PS(b0), cbf(C_ONE), brf[:, hf * 512:(hf + 1) * 512], start=True, stop=True),
                                     reads=['cstb', ('b_Br', di)], writes=[('ps', b0)])
                                S.op('act', lambda e, b0=b0, hf=hf: e.activation(out=eaf[:, hf * 512:(hf + 1) * 512], in_=PS(b0), func=AF.Exp),
                                     reads=[('ps', b0)], writes=[('b_Ea', di)])
                            S.op('dve', lambda e, di=di: e.tensor_tensor(out=Ew[di], in0=Ew[di], in1=CBm[di].unsqueeze(1).broadcast_to([128, 8, 128]), op=ALU.mult),
                                 reads=[('b_Ew', di), ('b_CBm', di)], writes=[('b_Ew', di)])
                            S.op('dve', lambda e, di=di: e.tensor_tensor(out=Ea[di], in0=Ea[di], in1=CcT[:, tc_].unsqueeze(1).broadcast_to([128, 8, 128]), op=ALU.mult),
                                 reads=[('b_Ea', di), 'b_CcT'], writes=[('b_Ea', di)])
                        yb = 4 + (t % 2)

                        def fy(e, t=t, yb=yb):
                            ins = None
                            for h in range(8):
                                hs = slice(h * 64, (h + 1) * 64)
                                e.matmul(PS(yb)[:, hs], Ew[0][:, h, :], xpf[:, t, hs], start=True, stop=False)
                                e.matmul(PS(yb)[:, hs], Ea[0][:, h, :], Sfbf[:, t, hs], start=False, stop=False)
                                e.matmul(PS(yb)[:, hs], Ew[1][:, h, :], xpb[:, hs], start=False, stop=False)
                                ins = e.matmul(PS(yb)[:, hs], Ea[1][:, h, :], Sbbf[:, hs], start=False, stop=True)
                            return ins
                        S.op('pe', fy, reads=[('b_Ew', 0), ('b_Ew', 1), ('b_Ea', 0), ('b_Ea', 1), ('b_xpf', t), 'b_xpb', ('b_Sfbf', t), 'b_Sbbf'],
                             writes=[('ps', yb)])
                        S.op('dve', lambda e, t=t: e.tensor_tensor(out=t1.rearrange("p (h q) -> p h q", q=64),
                                                                   in0=xtm[:, t, :].rearrange("p (h q) -> p h q", q=64),
                                                                   in1=dsk_b[:, 8 * g:8 * g + 8].unsqueeze(2).broadcast_to([128, 8, 64]), op=ALU.mult),
                             reads=['b_xtm', 'dsk'], writes=['b_t1'])
                        S.op('dve', lambda e, yb=yb: e.tensor_tensor(out=t1, in0=t1, in1=PS(yb), op=ALU.add),
                             reads=['b_t1', ('ps', yb)], writes=['b_t1'])
                        S.op('dve', lambda e, t=t: e.tensor_tensor(out=t2, in0=t1, in1=zs[:, t, :], op=ALU.mult),
                             reads=['b_t1', 'b_zs'], writes=['b_t2'])
                        ssq = small[:, 2:3]
                        S.op('act', lambda e: e.activation(out=t1, in_=t2, func=AF.Square, accum_out=ssq), reads=['b_t2'], writes=['b_t1', 'b_ssq'])
                        rsqrt_(ssq, ssq, 1.0 / 512, ['b_ssq'], ['b_ssq'])
                        S.op('dve', lambda e: e.scalar_tensor_tensor(out=ybf, in0=t2, scalar=ssq, in1=snw, op0=ALU.mult, op1=ALU.mult),
                             reads=['b_t2', 'b_ssq', 'b_snw'], writes=['b_ybf'])
                        bt = tbank()

                        def ft(e, bt=bt):
                            ins = None
                            for j in range(4):
                                ins = e.transpose(PSB(bt)[:, j * 128:(j + 1) * 128], ybf[:, j * 128:(j + 1) * 128], cbf(C_ID))
                            return ins
                        S.op('pe', ft, reads=['b_ybf', 'cstb'], writes=[('ps', bt)])
                        S.op('act', lambda e, t=t, bt=bt: e.activation(out=oT[:, 4 * g:4 * g + 4, t * 128:(t + 1) * 128],
                                                                     in_=PSB(bt)[:, 0:512].rearrange("p (j c) -> p j c", c=128), func=AF.Copy),
                             reads=[('ps', bt)], writes=['oT'])
                    state_update(1, t, g)
                if pre:
                    out_state(Sst[1], ('b_S', 1), sbp_ssm[g])
                else:
                    out_state(Sst[1], ('b_S', 1), ss_out[gseg[0], L, 1, g])

            for g in range(4):
                def f_snw(slot, g=g):
                    pass
                for j in range(4):
                    cbk = 4 * g + j
                    WS.add(w_in[L][:, BX0 + cbk * 128:BX0 + (cbk + 1) * 128], 32,
                           (lambda slot, cbk=cbk, j=j: conv_block(slot, cbk, xcT[:, j, :], ('b_xcT', j))))
                WS.add(w_in[L][:, BB0 + g * 128:BB0 + (g + 1) * 128], 32,
                       (lambda slot, g=g: conv_block(slot, 16 + g, BcT, 'b_BcT')))
                if not pre:
                    WS.add(w_in[L][:, BC0 + g * 128:BC0 + (g + 1) * 128], 32,
                           (lambda slot, g=g: conv_block(slot, 20 + g, CcT, 'b_CcT')))
                    for j in range(4):
                        def fz(slot, g=g, j=j):
                            b = tbank()
                            tm_mm(slot, 32, hT, HK, [0, 128, 256, 384], b)
                            S.op('act', lambda e: e.activation(out=zs[:, :, j * 128:(j + 1) * 128],
                                                               in_=PS(b).rearrange("p (t c) -> p t c", c=128), func=AF.Silu),
                                 reads=[('ps', b)], writes=['b_zs'])
                        c0 = BZ0 + (4 * g + j) * 128
                        WS.add(w_in[L][:, c0:c0 + 128], 32, fz)

                def fdt(slot, g=g):
                    b = tbank()
                    tm_mm(slot, 32, hT, HK, [0, 128, 256, 384], b, ncolw=16)
                    S.op('dve', lambda e: e.tensor_copy(out=dtbg[:, 0:8], in_=dtb_b[:, 8 * g:8 * g + 8]), reads=['dtb'], writes=['b_dtbg'])
                    S.op('dve', lambda e: e.tensor_copy(out=dtbg[:, 8:16], in_=dtb_b[:, 32 + 8 * g:40 + 8 * g]), reads=['dtb'], writes=['b_dtbg'])
                    S.op('dve', lambda e: e.tensor_copy(out=ag[:, 0:8], in_=a_b[:, 8 * g:8 * g + 8]), reads=['a_b'], writes=['b_ag'])
                    S.op('dve', lambda e: e.tensor_copy(out=ag[:, 8:16], in_=a_b[:, 32 + 8 * g:40 + 8 * g]), reads=['a_b'], writes=['b_ag'])
                    S.op('dve', lambda e: e.tensor_tensor(out=dtt, in0=PS(b)[:, 0:64].rearrange("p (t c) -> p t c", c=16),
                                                          in1=dtbg.unsqueeze(1).broadcast_to([128, 4, 16]), op=ALU.add),
                         reads=[('ps', b), 'b_dtbg'], writes=['b_dtt'])
                    S.op('act', lambda e: e.activation(out=dtt, in_=dtt, func=AF.Exp), reads=['b_dtt'], writes=['b_dtt'])
                    S.op('act', lambda e: e.activation(out=dtt, in_=dtt, func=AF.Ln, bias=cc[:, 1:2]), reads=['b_dtt'], writes=['b_dtt'])
                    S.op('dve', lambda e: e.tensor_tensor(out=dta, in0=dtt, in1=ag.unsqueeze(1).broadcast_to([128, 4, 16]), op=ALU.mult),
                         reads=['b_dtt', 'b_ag'], writes=['b_dta'])
                    if not pre:
                        S.dma('sp', snw, ssm_norm_w[L][g * 512:(g + 1) * 512].partition_broadcast(128), writes=['b_snw'])
                    ssd_group(g)
                WS.add([(w_in[L][:, BDT0 + 8 * g:BDT0 + 8 * g + 8], 0), (w_in[L][:, BDT0 + 32 + 8 * g:BDT0 + 40 + 8 * g], 8)], 32, fdt, ncol=16)
            WS.run()

        def branch_hgrn(P, mode):
            pre = (mode == 'pre')
            pi = P['pi']
            linkmid = flg[:, 0:1] if P['group'] else flg[:, 1:2]
            AR.reset()
            a1 = AR.get(512)
            a2 = AR.get(512)
            cum = AR.get(512)
            Pp = AR.get(512)
            Dd = AR.get(512)
            eX = AR.get(512)
            eP = AR.get(512)
            kk = AR.get(512)
            qs = AR.get(512)
            qt = AR.get(256, BF16)
            kt = AR.get(256, BF16)
            qh = AR.get(256, BF16)
            khT = AR.get(256, BF16)
            khtm = AR.get(256, BF16, (128, 4, 128))
            vtm = AR.get(256, BF16, (128, 4, 128))
            gs = AR.get(256, BF16)
            attm = AR.get(64, BF16)
            attf = AR.get(128)
            Sst = [AR.get(128), AR.get(128)]
            Sbf = AR.get(64, BF16)
            of = AR.get(512)
            osum = AR.get(512)
            sq = AR.get(256, BF16)
            rstd = AR.get(512)
            stS = [AR.get(128), AR.get(128)]
            sto = {'i': 0}
            QS = 128 ** -0.5

            def out_state(Ssrc, skey, dst):
                st = stS[sto['i'] % 2]
                sk = ('c_stS', sto['i'] % 2)
                sto['i'] += 1
                S.op('act', lambda e: e.activation(out=st, in_=Ssrc, func=AF.Copy), reads=[skey], writes=[sk])
                S.dma('sp', dst, st, reads=[sk], writes=[('sho', str(dst.offset))])

            def v3(a):
                return a.rearrange("p (c t) -> p c t", t=64)

            def dir_prep(b, di, h):
                lbc = lbT[:, L, di, h:h + 1]
                S.op('act', lambda e: e.activation(out=eX, in_=PS(b), func=AF.Exp, scale=-1.0), reads=[('ps', b)], writes=['c_eX'])
                S.op('act', lambda e: e.activation(out=a1, in_=eX, func=AF.Ln, bias=cc[:, 1:2]), reads=['c_eX'], writes=['c_a1'])
                S.op('act', lambda e: e.activation(out=a2, in_=eX, func=AF.Ln, bias=cc[:, 1:2], scale=lbc), reads=['c_eX', 'lbT'], writes=['c_a2'])
                S.op('dve', lambda e: e.tensor_tensor(out=a2, in0=a2, in1=a1, op=ALU.subtract), reads=['c_a1', 'c_a2'], writes=['c_a2'])
                S.op('act', lambda e: e.activation(out=a1, in_=a1, func=AF.Exp, scale=-1.0), reads=['c_a1'], writes=['c_a1'])
                S.op('dve', lambda e: e.tensor_scalar(out=kk, in0=a1, scalar1=nomlT[:, L, di, h:h + 1], scalar2=omlT[:, L, di, h:h + 1],
                                                      op0=ALU.mult, op1=ALU.add), reads=['c_a1', 'omlT', 'nomlT'], writes=['c_kk'])
                S.op('dve', lambda e: e.tensor_tensor_scan(out=cum, data0=cf(C_RST, 512), data1=a2, initial=0.0, op0=ALU.mult, op1=ALU.add),
                     reads=['cstf', 'c_a2'], writes=['c_cum'])
                if di == 0:
                    S.op('dve', lambda e: e.tensor_copy(out=Pp, in_=cum), reads=['c_cum'], writes=['c_Pp'])
                    mid, ti = 31, 63
                else:
                    S.op('dve', lambda e: e.tensor_tensor(out=Dd, in0=a2, in1=cum, op=ALU.subtract), reads=['c_a2', 'c_cum'], writes=['c_Dd'])
                    S.op('dve', lambda e: e.tensor_tensor(out=v3(Pp), in0=v3(Dd), in1=v3(cum)[:, :, 63:64].broadcast_to([128, 8, 64]), op=ALU.add),
                         reads=['c_Dd', 'c_cum'], writes=['c_Pp'])
                    mid, ti = 32, 0
                if not pre:
                    S.op('dve', lambda e: e.tensor_tensor(out=v3(Dd), in0=v3(Pp), in1=v3(Pp)[:, :, mid:mid + 1].broadcast_to([128, 8, 64]), op=ALU.subtract),
                         reads=['c_Pp'], writes=['c_Dd'])
                    S.op('act', lambda e: e.activation(out=eX, in_=Dd, func=AF.Exp), reads=['c_Dd'], writes=['c_eX'])
                    S.op('dve', lambda e: e.scalar_tensor_tensor(out=qt, in0=qs, scalar=QS, in1=eX, op0=ALU.mult, op1=ALU.mult),
                         reads=['c_qs', 'c_eX'], writes=['c_qt'])
                    S.op('act', lambda e: e.activation(out=eX, in_=Dd, func=AF.Exp, scale=-1.0), reads=['c_Dd'], writes=['c_eX'])
                    S.op('dve', lambda e: e.tensor_tensor(out=kt, in0=kk, in1=eX, op=ALU.mult), reads=['c_kk', 'c_eX'], writes=['c_kt'])
                S.op('act', lambda e: e.activation(out=eP, in_=Pp, func=AF.Exp), reads=['c_Pp'], writes=['c_eP'])
                if not pre:
                    S.op('dve', lambda e: e.scalar_tensor_tensor(out=qh, in0=qs, scalar=QS, in1=eP, op0=ALU.mult, op1=ALU.mult),
                         reads=['c_qs', 'c_eP'], writes=['c_qh'])
                S.op('dve', lambda e: e.tensor_tensor(out=v3(Dd), in0=v3(Pp)[:, :, ti:ti + 1].broadcast_to([128, 8, 64]), in1=v3(Pp), op=ALU.subtract),
                     reads=['c_Pp'], writes=['c_Dd'])
                S.op('act', lambda e: e.activation(out=eX, in_=Dd, func=AF.Exp), reads=['c_Dd'], writes=['c_eX'])
                S.op('dve', lambda e: e.tensor_tensor(out=khT, in0=kk, in1=eX, op=ALU.mult), reads=['c_kk', 'c_eX'], writes=['c_khT'])
                bt = tbank()

                def ft(e):
                    ins = None
                    for t in range(4):
                        ins = e.transpose(PSB(bt)[:, t * 128:(t + 1) * 128], khT[:, t * 128:(t + 1) * 128], cbf(C_ID))
                    return ins
                S.op('pe', ft, reads=['c_khT', 'cstb'], writes=[('ps', bt)])
                S.op('dve', lambda e: e.tensor_copy(out=khtm, in_=PSB(bt)[:, 0:512].rearrange("p (t c) -> p t c", c=128)),
                     reads=[('ps', bt)], writes=['c_khtm'])
                return ti

            def sweep(di, h, ti):
                fwd = (di == 0)
                gseg = P['segs']
                sk = ('c_S', di)
                if pre:
                    init_state(Sst[1], sk, 'dram', st_hg[L, 1, h], None)
                elif pi == 0:
                    if fwd:
                        init_state(Sst[0], sk, 'dram', st_hg[L, 0, h], None)
                    else:
                        init_state(Sst[1], sk, 'dram', sbp_hg[h], flg[:, 0:1])
                elif pi == 1:
                    if fwd:
                        init_state(Sst[0], sk, 'dram', sfc_hg[h], flg[:, 0:1])
                    else:
                        init_state(Sst[1], sk, 'dram', st_hg[L, 1, h], None)
                else:
                    init_state(Sst[di], sk, 'zero', None, None)
                St = Sst[di]
                if not pre:
                    S.op('act', lambda e: e.activation(out=Sbf, in_=St, func=AF.Copy), reads=[sk], writes=['c_Sbf'])
                tiles = (0, 1, 2, 3) if fwd else (3, 2, 1, 0)
                for t in tiles:
                    tcs = slice(t * 128, (t + 1) * 128)
                    ob = 4 + (t % 2) + (0 if fwd else 2)
                    if not pre:
                        b = tbank()
                        S.op('pe', lambda e, b=b, tcs=tcs: e.matmul(PS(b)[:, 0:128], kt[:, tcs], qt[:, tcs], start=True, stop=True),
                             reads=['c_kt', 'c_qt'], writes=[('ps', b)])
                        S.op('dve', lambda e, b=b: e.tensor_scalar(out=attf, in0=PS(b)[:, 0:128], scalar1=3.0e38, scalar2=-3.0e38,
                                                                  op0=ALU.min, op1=ALU.max),
                             reads=[('ps', b)], writes=['c_attf'])
                        S.op('dve', lambda e: e.tensor_tensor(out=attm, in0=attf, in1=cbf(C_BDF if fwd else C_BDB), op=ALU.mult),
                             reads=['c_attf', 'cstb'], writes=['c_attm'])
                        S.op('pe', lambda e, ob=ob, t=t: e.matmul(PS(ob)[:, 0:128], vtm[:, t, :], attm, start=True, stop=False),
                             reads=['c_vtm', 'c_attm'], writes=[('ps', ob)])
                    chunks = (2 * t, 2 * t + 1) if fwd else (2 * t + 1, 2 * t)
                    for ci, c in enumerate(chunks):
                        hf = c % 2
                        ccs = slice(c * 64, (c + 1) * 64)
                        ps_ = slice(hf * 64, (hf + 1) * 64)
                        if not pre:
                            S.op('pe', lambda e, ob=ob, ps_=ps_, ccs=ccs, ci=ci: e.matmul(PS(ob)[:, ps_], Sbf, qh[:, ccs], start=False, stop=(ci == 1)),
                                 reads=['c_Sbf', 'c_qh'], writes=[('ps', ob)])
                        b2 = tbank()
                        S.op('pe', lambda e, b2=b2, ps_=ps_, t=t: e.matmul(PS(b2)[:, 0:128], khtm[ps_, t, :], vtm[ps_, t, :], start=True, stop=True),
                             reads=['c_khtm', 'c_vtm'], writes=[('ps', b2)])
                        dcol = c * 64 + ti
                        S.op('dve', lambda e, b2=b2, dcol=dcol: e.scalar_tensor_tensor(out=St, in0=St, scalar=eP[:, dcol:dcol + 1], in1=PS(b2)[:, 0:128],
                                                                                      op0=ALU.mult, op1=ALU.add),
                             reads=[sk, 'c_eP', ('ps', b2)], writes=[sk])
                        seg_end = (c % 4 == 3) if fwd else (c % 4 == 0)
                        if seg_end:
                            sgi = gseg[c // 4]
                            if pre:
                                if c == 0:
                                    out_state(St, sk, sbp_hg[h])
                            else:
                                out_state(St, sk, sh_out[sgi, L, di, h])
                                if fwd and c == 7 and pi == 0:
                                    out_state(St, sk, sfc_hg[h])
                            if (fwd and c == 3) or ((not fwd) and c == 4):
                                S.op('dve', lambda e: e.tensor_scalar(out=St, in0=St, scalar1=linkmid, scalar2=None, op0=ALU.mult),
                                     reads=[sk, 'flg'], writes=[sk])
                        if not pre:
                            S.op('act', lambda e: e.activation(out=Sbf, in_=St, func=AF.Copy), reads=[sk], writes=['c_Sbf'])
                    if not pre:
                        if fwd:
                            S.op('act', lambda e, ob=ob, tcs=tcs: e.activation(out=of[:, tcs], in_=PS(ob)[:, 0:128], func=AF.Copy),
                                 reads=[('ps', ob)], writes=['c_of'])
                        else:
                            S.op('dve', lambda e, ob=ob, tcs=tcs: e.tensor_tensor(out=osum[:, tcs], in0=PS(ob)[:, 0:128], in1=of[:, tcs], op=ALU.add),
                                 reads=[('ps', ob), 'c_of'], writes=['c_osum'])

            for h in range(16):
                def fci(slot, h=h):
                    b = tbank()
                    tm_mm(slot, 32, hT, HK, [0, 128, 256, 384], b)
                    S.op('act', lambda e: e.activation(out=vtm, in_=PS(b).rearrange("p (t c) -> p t c", c=128), func=AF.Copy),
                         reads=[('ps', b)], writes=['c_vtm'])
                WS.add(w_in[L][:, CI0 + h * 128:CI0 + (h + 1) * 128], 32, fci)
                if not pre:
                    def fcq(slot, h=h):
                        b = tbank()
                        fm_mm(slot, 32, hT, HK, (0, 512), b)
                        S.op('act', lambda e: e.activation(out=qs, in_=PS(b), func=AF.Silu), reads=[('ps', b)], writes=['c_qs'])
                    WS.add(w_in[L][:, CQ0 + h * 128:CQ0 + (h + 1) * 128], 32, fcq)

                    def fcg(slot, h=h):
                        b = tbank()
                        fm_mm(slot, 32, hT, HK, (0, 512), b)
                        S.op('act', lambda e: e.activation(out=gs, in_=PS(b), func=AF.Silu), reads=[('ps', b)], writes=['c_gs'])
                    WS.add(w_in[L][:, CG0 + h * 128:CG0 + (h + 1) * 128], 32, fcg)

                    def fcf(slot, h=h):
                        b = tbank()
                        fm_mm(slot, 32, hT, HK, (0, 512), b)
                        ti = dir_prep(b, 0, h)
                        sweep(0, h, ti)
                    WS.add(w_in[L][:, CF0 + h * 128:CF0 + (h + 1) * 128], 32, fcf)

                def fcb(slot, h=h):
                    b = tbank()
                    fm_mm(slot, 32, hT, HK, (0, 512), b)
                    ti = dir_prep(b, 1, h)
                    sweep(1, h, ti)
                    if not pre:
                        S.op('act', lambda e: e.activation(out=sq, in_=osum, func=AF.Square), reads=['c_osum'], writes=['c_sq'])
                        b2 = tbank()
                        S.op('pe', lambda e: e.matmul(PS(b2), cbf(C_ONE), sq, start=True, stop=True), reads=['c_sq', 'cstb'], writes=[('ps', b2)])
                        rsqrt_(rstd, PS(b2), 1.0 / 128, [('ps', b2)], ['c_rstd'])
                        S.op('dve', lambda e: e.scalar_tensor_tensor(out=rstd, in0=osum, scalar=hnw[:, h:h + 1], in1=rstd, op0=ALU.mult, op1=ALU.mult),
                             reads=['c_osum', 'hnw', 'c_rstd'], writes=['c_rstd'])
                        S.op('dve', lambda e: e.tensor_tensor(out=oT[:, h, :], in0=rstd, in1=gs, op=ALU.mult), reads=['c_rstd', 'c_gs'], writes=['oT'])
                WS.add(w_in[L][:, CF0 + 2048 + h * 128:CF0 + 2048 + (h + 1) * 128], 32, fcb)
            WS.run()

        def branch_merge(i):
            AR.reset()
            sg = AR.get(512)
            tmp = AR.get(512)
            for db in range(32):
                st = {}

                def fb(slot, db=db, st=st):
                    b = tbank()
                    st['b'] = b
                    fm_mm(slot, 16, oT, 'oT', (0, 512), b)
                WS.add(w_bout[L, i][:, db * 128:(db + 1) * 128], 16, fb)

                def fgate(slot, db=db, st=st):
                    b2 = tbank()
                    fm_mm(slot, 32, hT, HK, (0, 512), b2)
                    S.op('act', lambda e: e.activation(out=sg, in_=PS(b2), func=AF.Sigmoid), reads=[('ps', b2)], writes=['m_sg'])
                    mk = ('mg', db // 16)
                    if i == branches[0]:
                        S.op('dve', lambda e: e.tensor_tensor(out=mgT[:, db, :], in0=PS(st['b']), in1=sg, op=ALU.mult),
                             reads=[('ps', st['b']), 'm_sg'], writes=[mk])
                    else:
                        S.op('dve', lambda e: e.tensor_tensor(out=tmp, in0=PS(st['b']), in1=sg, op=ALU.mult),
                             reads=[('ps', st['b']), 'm_sg'], writes=['m_tmp'])
                        S.op('dve', lambda e: e.tensor_tensor(out=mgT[:, db, :], in0=mgT[:, db, :], in1=tmp, op=ALU.add),
                             reads=['m_tmp', mk], writes=[mk])
                c0 = MG0 + i * D + db * 128
                WS.add(w_in[L][:, c0:c0 + 128], 32, fgate)
            WS.run()

        def out_phase(tiles, mrow):
            AR.reset()
            og = AR.get(512)
            xr = [AR.get(512), AR.get(512)]
            yst = [AR.get(512), AR.get(512)]
            row0 = tiles[0] * 128
            for db in range(32):
                def fo(slot, db=db):
                    b = tbank()
                    fm_mm(slot, 32, mgT, None, (0, 512), b)
                    S.op('act', lambda e: e.activation(out=og, in_=PS(b), func=AF.Copy, scale=modT[:, 64 + db, mrow:mrow + 1]),
                         reads=[('ps', b), 'modT'], writes=['o_og'])
                    b2 = tbank()

                    def f(e):
                        ins = None
                        for t in range(4):
                            ins = e.transpose(PS(b2)[:, t * 128:(t + 1) * 128], og[:, t * 128:(t + 1) * 128], cf(C_ID))
                        return ins
                    S.op('pe', f, reads=['o_og', 'cstf'], writes=[('ps', b2)])
                    xrt = xr[db % 2]
                    S.dma('sp', xrt.rearrange("p (t d) -> p t d", d=128),
                          xsrc[row0:row0 + 512, db * 128:(db + 1) * 128].rearrange("(t p) d -> p t d", p=128),
                          reads=[XK], writes=[('o_xr', db % 2)])
                    ys = yst[db % 2]
                    S.op('dve', lambda e: e.tensor_tensor(out=ys, in0=PS(b2), in1=xrt, op=ALU.add),
                         reads=[('ps', b2), ('o_xr', db % 2)], writes=[('o_ys', db % 2)])
                    S.dma('sp', xdst[row0:row0 + 512, db * 128:(db + 1) * 128].rearrange("(t p) d -> p t d", p=128),
                          ys.rearrange("p (t d) -> p t d", d=128), reads=[('o_ys', db % 2)], writes=[('yout', row0, db)])
                WS.add(w_o[L][:, db * 128:(db + 1) * 128], 32, fo)
            WS.run()


        passes = [
            dict(tiles=[0, 1, 2, 3], ext=4, ext_after=True, segs=[0, 1], mrow=1, group=True, kv=True),
            dict(tiles=[4, 5, 6, 7], ext=3, ext_after=False, segs=[2, 3], mrow=1, group=True, kv=False),
            dict(tiles=[8, 9, 10, 11], ext=None, ext_after=True, segs=[4, 5], mrow=0, group=False, kv=True),
        ]
        for pi, P in enumerate(passes):
            P['pi'] = pi
            P['tok0'] = P['tiles'][0] * 128
            if P['group']:
                P['nkt'] = 12
                P['krow0'] = 0
                P['bias'] = (lambda kt, s, pi=pi: abg[:, (2 * pi + s) * 12 + kt:(2 * pi + s) * 12 + kt + 1])
            else:
                P['nkt'] = 4
                P['krow0'] = 512 + P['tok0']
                P['bias'] = (lambda kt, s: abs_[:, s * 4 + kt:s * 4 + kt + 1])

        stage('mod')
        cache_step()
        S.barrier()
        stage('cache')
        P1 = passes[1]
        norm_phase(P1['tiles'], P1['ext'], P1['mrow'])
        stage('norm')
        kv_step(P1['tiles'], P1['segs'])
        S.barrier()
        stage('kv')
        if 1 in branches:
            branch_ssd(P1, 'pre')
            S.barrier()
        if 2 in branches:
            branch_hgrn(P1, 'pre')
            S.barrier()
        stage('pre')
        for pi, P in enumerate(passes):
            if pi not in cfg.get('passes', (0, 1, 2)):
                continue
            norm_phase(P['tiles'], P['ext'], P['mrow'])
            if P['kv']:
                kv_step(P['tiles'], P['segs'])
                S.barrier()
            first = True
            for br in branches:
                if br == 0:
                    branch_attn(P)
                elif br == 1:
                    branch_ssd(P, 'full')
                else:
                    branch_hgrn(P, 'full')
                S.barrier()
                stage('attn')
                branch_merge(br)
                S.barrier()
                stage('merge')
            out_phase(P['tiles'], P['mrow'])
            S.barrier()
            stage('out')

    with nc.allow_non_contiguous_dma(reason="small strided parameter / layout loads"):
        try:
            if cfg.get('stop') != 'consts':
                for L in range(nlayers):
                    layer(L)
        except (Stop, StopBuild):
            pass
        S.maxops = None
        S.finish()
    return nc, S


def _consts():
    p = np.arange(128)[:, None]
    f = np.arange(128)[None, :]
    c = np.zeros((128, C_END), np.float32)
    c[:, C_ID:C_ID + 128] = (p == f)
    c[:, C_ONE:C_ONE + 128] = 1.0
    c[:, C_MF:C_MF + 128] = (p <= f)
    c[:, C_MB:C_MB + 128] = (p >= f)
    c[:, C_SLF:C_SLF + 128] = (p > f)
    c[:, C_SUB:C_SUB + 128] = (p < f)
    rm = np.zeros((128, 128), np.float32)
    for base in (0, 64):
        for m in range(32):
            rm[base + 32 + m, base + m] = -1.0
            rm[base + m, base + 32 + m] = 1.0
    c[:, C_RM:C_RM + 128] = rm
    same = (p // 64) == (f // 64)
    c[:, C_BDF:C_BDF + 128] = (p <= f) & same
    c[:, C_BDB:C_BDB + 128] = (p >= f) & same
    t = np.arange(512)
    c[:, C_RST:C_RST + 512] = (t % 64 != 0).astype(np.float32)[None, :]
    return c


def _rope_tables(is_sample):
    cos = np.ones((128, NTOK), np.float32)
    sin = np.zeros((128, NTOK), np.float32)
    if is_sample:
        t = np.arange(1024)
        row = (t // 64).astype(np.float32)
        col = (t % 64).astype(np.float32)
        half = 64
        inv = (10000.0 ** (-np.arange(0, half, 2, dtype=np.float32) / half)).astype(np.float32)
        ra = row[None, :] * inv[:, None]
        ca = col[None, :] * inv[:, None]
        ang = np.concatenate([ra, ra, ca, ca], axis=0).astype(np.float32)
        cos[:, :1024] = np.cos(ang)
        sin[:, :1024] = np.sin(ang)
    return cos, sin


def _core_inputs(i, inp):
    is_sample = i < 4
    d = {}
    if is_sample:
        j = i
        segs_x = [inp['x_sample'][j].reshape(4, SEGL, D)[s] for s in range(4)] + [inp['x_prompt'][2 * j], inp['x_prompt'][2 * j + 1]]
        c1 = inp['c'][j]
        ck = np.ascontiguousarray(np.transpose(inp['cache_k'][j], (0, 2, 3, 1)))
        cv = np.ascontiguousarray(inp['cache_v'][j].reshape(DEPTH, 512, 512))
        ssm = inp['state_ssm'][j]
        ssm = np.ascontiguousarray(np.transpose(ssm.reshape(DEPTH, 2, 4, 512, 128), (0, 1, 2, 4, 3)))
        hg = np.ascontiguousarray(inp['state_hgrn'][j])
    else:
        base = 8 + 6 * (i - 4)
        segs_x = [inp['x_prompt'][base + s] for s in range(6)]
        c1 = inp['c_ctx']
        ck = np.zeros((DEPTH, 4, 128, 512), np.float32)
        cv = np.zeros((DEPTH, 512, 512), np.float32)
        ssm = np.zeros((DEPTH, 2, 4, 128, 512), np.float32)
        hg = np.zeros((DEPTH, 2, 16, 128, 128), np.float32)
    d['x'] = np.ascontiguousarray(np.concatenate(segs_x, axis=0))
    d['c2'] = np.ascontiguousarray(np.stack([inp['c_ctx'], c1], axis=0))
    d['cache_k'] = ck
    d['cache_v'] = cv
    d['st_ssm'] = ssm
    d['st_hg'] = hg
    d['cst'] = _consts()
    cos, sin = _rope_tables(is_sample)
    d['ropec'] = cos
    d['ropes'] = sin
    ab = np.zeros((4, 12), np.float32)
    if not is_sample:
        ab[:] = NEG
        for a in range(4):
            ab[a, 4 + 2 * a] = 0.0
            ab[a, 5 + 2 * a] = 0.0
    d['abias_g'] = np.ascontiguousarray(np.broadcast_to(ab.reshape(1, 48), (128, 48))).astype(np.float32)
    ab2 = np.full((2, 4), NEG, np.float32)
    for s in range(2):
        ab2[s, 2 * s] = 0.0
        ab2[s, 2 * s + 1] = 0.0
    d['abias_s'] = np.ascontiguousarray(np.broadcast_to(ab2.reshape(1, 8), (128, 8))).astype(np.float32)
    fl = np.zeros((128, 4), np.float32)
    fl[:, 0] = 1.0 if is_sample else 0.0
    fl[:, 2] = 1.0
    d['flags'] = fl
    for k in ('ln_w', 'w_mod', 'b_mod', 'w_in', 'q_norm_w', 'k_norm_w', 'conv_w', 'conv_b', 'd_skip',
              'ssm_norm_w', 'hgrn_lb', 'hgrn_norm_w', 'w_bout', 'w_o'):
        d[k] = np.ascontiguousarray(inp[k])
    d['dt_bias'] = np.ascontiguousarray(inp['dt_bias'].reshape(DEPTH, 64))
    d['a_log'] = np.ascontiguousarray(inp['a_log'].reshape(DEPTH, 64))
    return d


def _assemble(results):
    y_prompt = np.zeros((32, SEGL, D), np.float32)
    y_sample = np.zeros((4, 1024, D), np.float32)
    nck = np.zeros((32, DEPTH, SEGL, 4, 128), np.float32)
    ncv = np.zeros((32, DEPTH, SEGL, 4, 128), np.float32)
    nss = np.zeros((32, DEPTH, 2, 32, 64, 128), np.float32)
    nsh = np.zeros((32, DEPTH, 2, 16, 128, 128), np.float32)
    for i, r in enumerate(results):
        y = r['y'].reshape(NSEG, SEGL, D)
        if i < 4:
            y_sample[i] = y[0:4].reshape(1024, D)
            pmap = {4: 2 * i, 5: 2 * i + 1}
        else:
            pmap = {s: 8 + 6 * (i - 4) + s for s in range(6)}
        for s, b in pmap.items():
            y_prompt[b] = y[s]
            nck[b] = np.transpose(r['ck'][s], (0, 3, 1, 2))
            ncv[b] = r['cv'][s].reshape(DEPTH, SEGL, 4, 128)
            ss = r['ss'][s]
            nss[b] = np.transpose(ss, (0, 1, 2, 4, 3)).reshape(DEPTH, 2, 32, 64, 128)
            nsh[b] = r['sh'][s]
    return (y_prompt, y_sample, nck, ncv, nss, nsh)


_CACHE = {}


def kernel(**inputs):
    inp = {k: np.asarray(v) for k, v in inputs.items()}
    if 'nc' not in _CACHE:
        _CACHE['nc'] = build_program({})[0]
    nc = _CACHE['nc']
    in_maps = [_core_inputs(i, inp) for i in range(NCORE)]
    res = run_bass_kernel_spmd(nc, in_maps, core_ids=list(range(NCORE)))
    return _assemble(res.results)
```

```python
import numpy as np
import concourse.bass as bass
import concourse.mybir as mybir
from concourse.bass_utils import run_bass_kernel_spmd

F32 = mybir.dt.float32
BF16 = mybir.dt.bfloat16
AF = mybir.ActivationFunctionType
ALU = mybir.AluOpType
AX = mybir.AxisListType

D = 4096
DEPTH = 2
NCORE = 8
NSEG = 6
SEGL = 256
NTOK = NSEG * SEGL
PT = 512
EPS = 1e-6
AQ0, AK0, AV0, AG0 = 0, 2048, 2560, 3072
BX0, BZ0, BB0, BC0, BDT0 = 5120, 7168, 9216, 9728, 10240
CQ0, CF0, CI0, CG0, MG0 = 10304, 12352, 16448, 18496, 20544
N_IN = 32832
NEG = -30000.0

C_ID, C_ONE, C_MF, C_MB, C_SLF, C_SUB, C_RM, C_BDF, C_BDB = [i * 128 for i in range(9)]
C_RST = 9 * 128
C_END = C_RST + 512


class StopBuild(Exception):
    pass


class Sched:
    def __init__(self, nc):
        self.nc = nc
        self.E = {'pe': nc.tensor, 'dve': nc.vector, 'act': nc.scalar, 'pool': nc.gpsimd, 'sp': nc.sync}
        self.sems = {}
        for e in self.E:
            self.sems[e] = nc.alloc_semaphore('s_' + e)
        self.cnt = {e: 0 for e in self.E}
        self.seen = {e: {} for e in self.E}
        self.lastw = {}
        self.rd = {}
        self.dq = {}
        for q, n in (('sp', 16), ('pool', 8)):
            ids = []
            for i in range(n):
                sid = 'd_%s%d' % (q, i)
                self.sems[sid] = nc.alloc_semaphore(sid)
                ids.append(sid)
            self.dq[q] = dict(ids=ids, vals=[0] * n, idx=0)
        self.nops = 0
        self.maxops = None
        self.noself = False

    def _wait(self, e, need):
        for sid, val in need.items():
            if self.seen[e].get(sid, 0) < val:
                self.E[e].wait_ge(self.sems[sid], val)
                self.seen[e][sid] = val

    def _deps(self, e, reads, writes):
        need = {}

        def add(sid, val):
            if sid == 'pe' and e == 'pe':
                return
            if self.noself and sid == e:
                return
            if need.get(sid, 0) < val:
                need[sid] = val
        for k in reads:
            if k in self.lastw:
                add(*self.lastw[k])
            if isinstance(k, tuple) and k[0] == 'ps':
                for sid, val in self.rd.get(k, {}).items():
                    if sid != e:
                        add(sid, val)
        for k in writes:
            if k in self.lastw:
                add(*self.lastw[k])
            for sid, val in self.rd.get(k, {}).items():
                add(sid, val)
        return need

    def _commit(self, tok, reads, writes):
        sid, val = tok
        for k in reads:
            d = self.rd.setdefault(k, {})
            if d.get(sid, 0) < val:
                d[sid] = val
        for k in writes:
            self.lastw[k] = tok
            self.rd[k] = {}

    def op(self, e, fn, reads=(), writes=()):
        if self.maxops is not None and self.nops >= self.maxops:
            raise StopBuild()
        self._wait(e, self._deps(e, reads, writes))
        ins = fn(self.E[e])
        self.cnt[e] += 1
        ins.then_inc(self.sems[e], 1)
        self._commit((e, self.cnt[e]), reads, writes)
        self.nops += 1

    def dma(self, q, out, in_, reads=(), writes=(), maxlast=None):
        if self.maxops is not None and self.nops >= self.maxops:
            raise StopBuild()
        need = self._deps(q, reads, writes)
        d = self.dq[q]
        i = d['idx']
        d['idx'] = (i + 1) % len(d['ids'])
        sid = d['ids'][i]
        if d['vals'][i] > 0:
            need[sid] = max(need.get(sid, 0), d['vals'][i])
        self._wait(q, need)
        if maxlast is None:
            self.E[q].dma_start(out=out, in_=in_, allow_slow_non_contiguous=True).then_inc(self.sems[sid], 16)
        else:
            self.E[q].dma_start(out=out, in_=in_, max_dma_last_dim=maxlast).then_inc(self.sems[sid], 16)
        d['vals'][i] += 16
        self._commit((sid, d['vals'][i]), reads, writes)
        self.nops += 1

    def barrier(self):
        tgt = {}
        for e in self.E:
            if self.cnt[e] > 0:
                tgt[e] = self.cnt[e]
        for q in self.dq:
            d = self.dq[q]
            for sid, v in zip(d['ids'], d['vals']):
                if v > 0:
                    tgt[sid] = v
        for e in self.E:
            self._wait(e, dict(tgt))
        self.lastw = {}
        self.rd = {}

    def finish(self):
        self.barrier()


def build_program(cfg):
    nlayers = cfg.get('nlayers', DEPTH)
    LW = cfg.get('wlayers', DEPTH)
    branches = cfg.get('branches', (0, 1, 2))
    nc = bass.Bass("TRN2", target_bir_lowering=False)
    S = Sched(nc)
    S.maxops = cfg.get('maxops')
    S.noself = cfg.get('noself', False)

    def din(name, shape, dt=F32):
        return nc.dram_tensor(name, list(shape), dt, kind="ExternalInput").ap()

    def dout(name, shape, dt=F32):
        return nc.dram_tensor(name, list(shape), dt, kind="ExternalOutput").ap()

    def dscr(name, shape, dt=F32):
        return nc.dram_tensor(name, list(shape), dt, kind="Internal").ap()

    x_in = din("x", [NTOK, D])
    c2 = din("c2", [2, D])
    cache_k = din("cache_k", [DEPTH, 4, 128, 512])
    cache_v = din("cache_v", [DEPTH, 512, 512])
    st_ssm = din("st_ssm", [DEPTH, 2, 4, 128, 512])
    st_hg = din("st_hg", [DEPTH, 2, 16, 128, 128])
    cst = din("cst", [128, C_END])
    ropec = din("ropec", [128, NTOK])
    ropes = din("ropes", [128, NTOK])
    abias_g = din("abias_g", [128, 4 * 12])
    abias_s = din("abias_s", [128, 2 * 4])
    flags = din("flags", [128, 4])
    ln_w = din("ln_w", [DEPTH, D])
    TINY = cfg.get('tiny', False)
    w_mod = din("w_mod", [LW, 96, 128, 32 * 128])
    b_mod = din("b_mod", [DEPTH, 3 * D])
    w_in = din("w_in", [LW, 256, 128, 32 * 128])
    w_dt = din("w_dt", [LW, D, 64])
    q_norm_w = din("q_norm_w", [DEPTH, 128])
    k_norm_w = din("k_norm_w", [DEPTH, 128])
    conv_w = din("conv_w", [DEPTH, 3, 3072])
    conv_b = din("conv_b", [DEPTH, 3072])
    dt_bias = din("dt_bias", [DEPTH, 64])
    a_log = din("a_log", [DEPTH, 64])
    d_skip = din("d_skip", [DEPTH, 32])
    ssm_norm_w = din("ssm_norm_w", [DEPTH, 2048])
    hgrn_lb = din("hgrn_lb", [DEPTH, 2, 2048])
    hgrn_norm_w = din("hgrn_norm_w", [DEPTH, 2048])
    w_bout = din("w_bout", [LW, 3, 32, 128, 16 * 128])
    w_o = din("w_o", [LW, 32, 128, 32 * 128])

    y_out = dout("y", [NTOK, D])
    ck_out = dout("ck", [NSEG, DEPTH, 4, 128, SEGL])
    cv_out = dout("cv", [NSEG, DEPTH, SEGL, 512])
    ss_out = dout("ss", [NSEG, DEPTH, 2, 4, 128, 512])
    sh_out = dout("sh", [NSEG, DEPTH, 2, 16, 128, 128])

    kT_scr = dscr("kT_scr", [4, 128, 2048], BF16)
    v_scr = dscr("v_scr", [2048, 512], BF16)
    sbp_ssm = dscr("sbp_ssm", [4, 128, 512])
    sbp_hg = dscr("sbp_hg", [16, 128, 128])
    sfc_ssm = dscr("sfc_ssm", [4, 128, 512])
    sfc_hg = dscr("sfc_hg", [16, 128, 128])
    x1_scr = dscr("x1_scr", [NTOK, D])

    def sb(name, shape, dt=F32):
        return nc.alloc_sbuf_tensor(name, list(shape), dt)

    cstf = sb("cstf", [128, C_END])
    cstb = sb("cstb", [128, C_END], BF16)
    cc = sb("cc", [128, 8])
    ropec_t = sb("ropec_t", [128, NTOK], BF16)
    ropes_t = sb("ropes_t", [128, NTOK], BF16)
    abg = sb("abg", [128, 48])
    abs_ = sb("abs_", [128, 8])
    flg = sb("flg", [128, 4])
    cT = sb("cT", [128, 32, 2])
    scT = sb("scT", [128, 32, 2], BF16)
    lnwT = sb("lnwT", [128, 32])
    bmodT = sb("bmodT", [128, 96, 2])
    modT = sb("modT", [128, 96, 2])
    g1 = sb("g1", [128, 32, 2])
    qw = sb("qw", [128, 1])
    kw = sb("kw", [128, 1])
    cw = sb("cw", [128, 24, 3])
    cwl = sb("cwl", [128, 24, 2])
    cb = sb("cb", [128, 24])
    dtb_b = sb("dtb_b", [128, 64])
    a_b = sb("a_b", [128, 64])
    dsk_b = sb("dsk_b", [128, 32])
    hnw = sb("hnw", [128, 16])
    lbraw = sb("lbraw", [128, 2, 2, 16])
    lbe = sb("lbe", [128, 2, 2, 16])
    lbs = sb("lbs", [128, 2, 16])
    lbT = sb("lbT", [128, 2, 2, 16])
    omlT = sb("omlT", [128, 2, 2, 16])
    nomlT = sb("nomlT", [128, 2, 2, 16])
    hT = sb("hT", [128, 32, PT], BF16)
    hTe = sb("hTe", [128, 32, 128], BF16)
    mgT = sb("mgT", [128, 32, PT], BF16)
    oT = sb("oT", [128, 16, PT], BF16)
    NSLOT = 4
    ring = [sb("ring%d" % i, [128, 32, 128], BF16) for i in range(NSLOT)]
    work = sb("work", [128, 15360])
    small = sb("small", [128, 64])

    ps_all = nc.alloc_psum_tensor("ps_all", [128, 8, 512], F32) if hasattr(nc, 'alloc_psum_tensor') else None
    assert ps_all is not None

    def PS(b):
        return ps_all[:, b, :]

    def PSB(b):
        return ps_all[:, b, :].bitcast(BF16)

    rot = {'i': 0}

    def tbank():
        b = rot['i'] % 4
        rot['i'] += 1
        return b

    def cf(col, n=128):
        return cstf[:, col:col + n]

    def cbf(col, n=128):
        return cstb[:, col:col + n]

    xb = [mgT[:, 0:16, :].rearrange("p a b -> p (a b)").bitcast(F32),
          mgT[:, 16:32, :].rearrange("p a b -> p (a b)").bitcast(F32)]
    xn = oT[:, 0:8, :].rearrange("p a b -> p (a b)")

    class Arena:
        def __init__(self):
            self.off = 0

        def reset(self):
            self.off = 0

        def get(self, words_f32, dt=F32, shape=None):
            a = work[:, self.off:self.off + words_f32]
            self.off += words_f32
            assert self.off <= 15360, self.off
            if dt == BF16:
                a = a.bitcast(BF16)
            if shape is not None:
                if len(shape) == 3:
                    a = a.rearrange("p (a b) -> p a b", b=shape[2])
                elif len(shape) == 4:
                    a = a.rearrange("p (a b c) -> p a b c", b=shape[2], c=shape[3])
            return a
    AR = Arena()

    class WStream:
        def __init__(self):
            self.items = []

        def add(self, wsrc, nk, fn, ncol=128):
            self.items.append((wsrc, nk, fn, ncol))

        def run(self, depth=NSLOT - 1):
            items = self.items
            self.items = []
            n = len(items)
            issued = 0
            st = WStream.state

            def issue(j):
                wsrc, nk, fn, ncol = items[j]
                if wsrc is None:
                    return None
                slot = st['n'] % NSLOT
                st['n'] += 1
                if isinstance(wsrc, list):
                    for ap, c0 in wsrc:
                        w = ap.shape[1]
                        S.dma('pool', ring[slot][:, 0:nk, c0:c0 + w],
                              ap.rearrange("(k p) c -> p k c", p=128),
                              reads=(), writes=[('ring', slot)])
                else:
                    S.dma('pool', ring[slot][:, 0:nk, :].rearrange("p k c -> p (k c)"), wsrc,
                          reads=(), writes=[('ring', slot)], maxlast=8192)
                return slot
            slots = {}
            for i in range(n):
                while issued < n and issued <= i + depth:
                    pending = [j for j in range(i, issued) if slots.get(j) is not None]
                    if items[issued][0] is not None and len(pending) >= NSLOT:
                        break
                    if isinstance(items[issued][0], list) and len(pending) >= 2:
                        break
                    slots[issued] = issue(issued)
                    issued += 1
                sl = slots[i]
                items[i][2](None if sl is None else (ring[sl], ('ring', sl)))
    WStream.state = {'n': 0}

    def win(L_, c0):
        c1 = c0 if c0 < BDT0 else c0 - 64
        assert c1 % 128 == 0, c0
        return w_in[L_, c1 // 128]
    WS = WStream()

    def fm_mm(slot, nk, src, srckey, cols, bank, ncolw=128):
        rt, rk = slot
        c0, n = cols

        def f(e):
            ins = None
            for k in range(nk):
                ins = e.matmul(PS(bank)[0:ncolw, 0:n], rt[:, k, 0:ncolw], src[:, k, c0:c0 + n],
                               start=(k == 0), stop=(k == nk - 1))
            return ins
        keys = [rk] + ([('mg', 0), ('mg', 1)] if srckey is None else [srckey])
        S.op('pe', f, reads=keys, writes=[('ps', bank)])

    def tm_mm(slot, nk, src, srckey, tiles, bank, ncolw=128):
        rt, rk = slot

        def f(e):
            ins = None
            for j, t0 in enumerate(tiles):
                for k in range(nk):
                    ins = e.matmul(PS(bank)[:, j * ncolw:(j + 1) * ncolw], src[:, k, t0:t0 + 128],
                                   rt[:, k, 0:ncolw], start=(k == 0), stop=(k == nk - 1))
            return ins
        S.op('pe', f, reads=[rk, srckey], writes=[('ps', bank)])

    def rsqrt_(dst, src, scale, rk, wk):
        S.op('act', lambda e: e.activation(out=dst, in_=src, func=AF.Ln, bias=cc[:, 0:1], scale=scale),
             reads=rk, writes=wk)
        S.op('act', lambda e: e.activation(out=dst, in_=dst, func=AF.Exp, scale=-0.5), reads=wk, writes=wk)

    stg = sb("stg", [128, 128])

    def loadT(dst, src1d, n, wkey):
        S.dma('sp', stg[0:n, :], src1d.rearrange("(k p) -> k p", p=128), writes=['stg'])
        b = tbank()
        S.op('pe', lambda e: e.transpose(PS(b)[:, 0:n], stg[0:n, :], cstf[0:n, C_ID:C_ID + n]),
             reads=['stg', 'cstf'], writes=[('ps', b)])
        S.op('dve', lambda e: e.tensor_copy(out=dst, in_=PS(b)[:, 0:n]), reads=[('ps', b)], writes=[wkey])

    class Stop(Exception):
        pass

    def stage(name):
        if cfg.get('stop') == name:
            raise Stop()

    S.dma('sp', cstf[:, :], cst[:, :], writes=['cstf'])
    S.op('dve', lambda e: e.tensor_copy(out=cstb[:, :], in_=cstf[:, :]), reads=['cstf'], writes=['cstb'])
    S.op('dve', lambda e: e.memset(cc[:, 0:1], EPS), writes=['cc'])
    S.op('dve', lambda e: e.memset(cc[:, 1:2], 1.0), writes=['cc'])
    S.op('dve', lambda e: e.memset(cc[:, 2:3], 0.0), writes=['cc'])
    tmpr = work[:, 0:NTOK]
    S.dma('sp', tmpr, ropec[:, :], writes=['tmpr'])
    S.op('dve', lambda e: e.tensor_copy(out=ropec_t[:, :], in_=tmpr), reads=['tmpr'], writes=['rope'])
    tmpr2 = work[:, NTOK:2 * NTOK]
    S.dma('sp', tmpr2, ropes[:, :], writes=['tmpr2'])
    S.op('dve', lambda e: e.tensor_copy(out=ropes_t[:, :], in_=tmpr2), reads=['tmpr2'], writes=['rope'])
    S.dma('sp', abg[:, :], abias_g[:, :], writes=['abg'])
    S.dma('sp', abs_[:, :], abias_s[:, :], writes=['abs'])
    S.dma('sp', flg[:, :], flags[:, :], writes=['flg'])
    for l_ in range(2):
        for d_ in range(2):
            loadT(lbraw[:, l_, d_, :], hgrn_lb[l_, d_], 16, 'lbraw')
    S.op('act', lambda e: e.activation(out=lbe[:, :, :, :], in_=lbraw[:, :, :, :], func=AF.Exp),
         reads=['lbraw'], writes=['lbe'])
    S.op('dve', lambda e: e.tensor_tensor(out=lbs[:, :, :], in0=lbe[:, 0, :, :], in1=lbe[:, 1, :, :], op=ALU.add),
         reads=['lbe'], writes=['lbs'])
    S.op('dve', lambda e: e.reciprocal(out=lbs[:, :, :], in_=lbs[:, :, :]), reads=['lbs'], writes=['lbs'])
    S.op('dve', lambda e: e.tensor_tensor(out=lbe[:, 0, :, :], in0=lbe[:, 0, :, :], in1=lbs[:, :, :], op=ALU.mult),
         reads=['lbe', 'lbs'], writes=['lbe'])
    S.op('dve', lambda e: e.tensor_tensor(out=lbe[:, 1, :, :], in0=lbe[:, 1, :, :], in1=lbs[:, :, :], op=ALU.mult),
         reads=['lbe', 'lbs'], writes=['lbe'])
    S.op('dve', lambda e: e.tensor_tensor(out=lbT[:, 0, :, :], in0=lbe[:, 0, :, :], in1=lbe[:, 0, :, :], op=ALU.subtract),
         reads=['lbe'], writes=['lbT'])
    S.op('dve', lambda e: e.tensor_tensor(out=lbT[:, 1, :, :], in0=lbe[:, 0, :, :], in1=lbe[:, 1, :, :], op=ALU.add),
         reads=['lbe'], writes=['lbT'])
    S.op('dve', lambda e: e.tensor_tensor(out=lbT[:, 1, :, :], in0=lbT[:, 1, :, :], in1=lbe[:, 0, :, :], op=ALU.subtract),
         reads=['lbe', 'lbT'], writes=['lbT'])
    S.op('dve', lambda e: e.tensor_scalar(out=omlT[:, :, :, :], in0=lbT[:, :, :, :], scalar1=-1.0, scalar2=1.0,
                                          op0=ALU.mult, op1=ALU.add), reads=['lbT'], writes=['omlT'])
    S.op('dve', lambda e: e.tensor_scalar(out=nomlT[:, :, :, :], in0=lbT[:, :, :, :], scalar1=-1.0, scalar2=None,
                                          op0=ALU.add), reads=['lbT'], writes=['nomlT'])
    if TINY:
        for i_ in range(NSLOT):
            S.op('dve', lambda e, i_=i_: e.memset(ring[i_][:, :, :], 0.0), writes=[('ring', i_)])
    S.barrier()
    def layer(L):
        xsrc = x_in if L == 0 else x1_scr
        xdst = y_out if L == nlayers - 1 else x1_scr
        XK = ('xres',)

        for r in range(2):
            loadT(cT[:, :, r], c2[r], 32, 'cT')
        loadT(lnwT[:, :], ln_w[L], 32, 'lnwT')
        for r in range(2):
            loadT(bmodT[:, :, r], b_mod[L], 96, 'bmodT')
        S.dma('sp', qw[:, :], q_norm_w[L].rearrange("(p o) -> p o", o=1), writes=['qw'])
        S.dma('sp', kw[:, :], k_norm_w[L].rearrange("(p o) -> p o", o=1), writes=['kw'])
        for t in range(3):
            loadT(cw[:, :, t], conv_w[L, t], 24, 'cw')
        loadT(cb[:, :], conv_b[L], 24, 'cb')
        S.dma('sp', dtb_b[:, :], dt_bias[L].partition_broadcast(128), writes=['dtb'])
        S.dma('sp', a_b[:, :], a_log[L].partition_broadcast(128), writes=['a_b'])
        S.dma('sp', dsk_b[:, :], d_skip[L].partition_broadcast(128), writes=['dsk'])
        loadT(hnw[:, :], hgrn_norm_w[L], 16, 'hnw')
        S.op('act', lambda e: e.activation(out=a_b[:, :], in_=a_b[:, :], func=AF.Exp), reads=['a_b'], writes=['a_b'])
        S.op('dve', lambda e: e.tensor_scalar(out=a_b[:, :], in0=a_b[:, :], scalar1=-1.0, scalar2=None, op0=ALU.mult),
             reads=['a_b'], writes=['a_b'])
        S.op('dve', lambda e: e.tensor_scalar(out=cwl[:, :, 0], in0=cw[:, :, 0], scalar1=flg[:, 0:1], scalar2=None,
                                              op0=ALU.mult), reads=['cw', 'flg'], writes=['cwl'])
        S.op('dve', lambda e: e.tensor_scalar(out=cwl[:, :, 1], in0=cw[:, :, 2], scalar1=flg[:, 0:1], scalar2=None,
                                              op0=ALU.mult), reads=['cw', 'flg'], writes=['cwl'])
        S.op('act', lambda e: e.activation(out=scT[:, :, :], in_=cT[:, :, :], func=AF.Silu), reads=['cT'], writes=['scT'])

        stage('params')
        MB = 7
        for cbk in range(96):
            def fn(slot, cbk=cbk):
                rt, rk = slot

                def f(e):
                    ins = None
                    for k in range(32):
                        ins = e.matmul(PS(MB)[:, cbk * 2:cbk * 2 + 2], rt[:, k, :], scT[:, k, :],
                                       start=(k == 0), stop=(k == 31))
                    return ins
                S.op('pe', f, reads=[rk, 'scT'], writes=[('ps', MB)])
            WS.add(w_mod[L, cbk], 32, fn)
        WS.run()
        S.op('dve', lambda e: e.tensor_tensor(out=modT[:, :, :], in0=PS(MB)[:, 0:192].rearrange("p (a b) -> p a b", b=2),
                                              in1=bmodT[:, :, :], op=ALU.add),
             reads=[('ps', MB), 'bmodT'], writes=['modT'])
        S.op('dve', lambda e: e.scalar_tensor_tensor(out=g1[:, :, :], in0=modT[:, 32:64, :], scalar=1.0,
                                                     in1=lnwT[:, :].unsqueeze(2).broadcast_to([128, 32, 2]),
                                                     op0=ALU.add, op1=ALU.mult),
             reads=['modT', 'lnwT'], writes=['g1'])

        def norm_tile(i, row0, dst, dcol, mrow):
            xt = xb[i % 2]
            xk = ('mg', i % 2)
            S.dma('sp', xt, xsrc[row0:row0 + 128, :], reads=[XK], writes=[xk])
            ss = small[:, 0:1]
            S.op('act', lambda e: e.activation(out=xn, in_=xt, func=AF.Square, accum_out=ss),
                 reads=[xk], writes=['oT', 'ss'])
            rsqrt_(ss, ss, 1.0 / D, ['ss'], ['ss'])
            S.op('dve', lambda e: e.tensor_scalar(out=xn, in0=xt, scalar1=ss, scalar2=None, op0=ALU.mult),
                 reads=[xk, 'ss'], writes=['oT'])
            for q in range(4):
                b = tbank()

                def f(e, q=q, b=b):
                    ins = None
                    for j in range(8):
                        kc = q * 8 + j
                        ins = e.transpose(PSB(b)[:, j * 128:(j + 1) * 128], xn[:, kc * 128:(kc + 1) * 128], cbf(C_ID))
                    return ins
                S.op('pe', f, reads=['oT', 'cstb'], writes=[('ps', b)])
                for j in range(8):
                    kc = q * 8 + j
                    S.op('act', lambda e, j=j, kc=kc, b=b: e.activation(
                        out=dst[:, kc, dcol:dcol + 128], in_=PSB(b)[:, j * 128:(j + 1) * 128], func=AF.Identity,
                        scale=g1[:, kc, mrow:mrow + 1], bias=modT[:, kc, mrow:mrow + 1]),
                        reads=[('ps', b), 'g1', 'modT'], writes=[('hT', id(dst))])

        def norm_phase(tiles, ext_tile, mrow):
            for i, t in enumerate(tiles):
                norm_tile(i, t * 128, hT, i * 128, mrow)
            if ext_tile is not None:
                norm_tile(len(tiles), ext_tile * 128, hTe, 0, mrow)
        HK = ('hT', id(hT))
        HEK = ('hT', id(hTe))

        def kv_step(tiles, segs):
            AR.reset()
            raw = AR.get(512)
            sq = AR.get(256, BF16)
            rstd = AR.get(512)
            knf = AR.get(512)
            knb = AR.get(256, BF16)
            t1 = AR.get(512)
            krp = AR.get(256, BF16)
            vst = AR.get(512)
            vbf = AR.get(256, BF16)
            tok0 = tiles[0] * 128
            for g in range(4):
                def fk(slot, g=g):
                    if cfg.get('kvbank'):
                        rot['i'] = cfg['kvbank']
                    b = tbank()
                    fm_mm(slot, 32, hT, HK, (0, 512), b)
                    S.op('act', lambda e: e.activation(out=sq, in_=PS(b), func=AF.Square), reads=[('ps', b)], writes=['kv_sq'])
                    S.op('dve', lambda e: e.tensor_copy(out=raw, in_=PS(b)), reads=[('ps', b)], writes=['kv_raw'])
                    b2 = tbank()
                    S.op('pe', lambda e: e.matmul(PS(b2), cbf(C_ONE), sq, start=True, stop=True),
                         reads=['kv_sq', 'cstb'], writes=[('ps', b2)])
                    rsqrt_(rstd, PS(b2), 1.0 / 128, [('ps', b2)], ['kv_rstd'])
                    S.op('dve', lambda e: e.scalar_tensor_tensor(out=knf, in0=raw, scalar=kw[:, 0:1], in1=rstd,
                                                                 op0=ALU.mult, op1=ALU.mult),
                         reads=['kv_raw', 'kw', 'kv_rstd'], writes=['kv_knf'])
                    if cfg.get('kvl', 9) < 2:
                        return
                    for si, sg in enumerate(segs):
                        S.dma('sp', ck_out[sg, L, g, :, :], knf[:, si * 256:(si + 1) * 256], reads=['kv_knf'], writes=[('ck', sg, g)])
                    if cfg.get('kvl', 9) < 3:
                        return
                    S.op('act', lambda e: e.activation(out=knb, in_=knf, func=AF.Copy), reads=['kv_knf'], writes=['kv_knb'])
                    b3 = tbank()
                    S.op('pe', lambda e: e.matmul(PS(b3), cbf(C_RM), knb, start=True, stop=True),
                         reads=['kv_knb', 'cstb'], writes=[('ps', b3)])
                    S.op('dve', lambda e: e.tensor_tensor(out=t1, in0=knb, in1=ropec_t[:, tok0:tok0 + 512], op=ALU.mult),
                         reads=['kv_knb', 'rope'], writes=['kv_t1'])
                    S.op('dve', lambda e: e.tensor_tensor(out=rstd, in0=PS(b3), in1=ropes_t[:, tok0:tok0 + 512], op=ALU.mult),
                         reads=[('ps', b3), 'rope'], writes=['kv_rstd'])
                    S.op('dve', lambda e: e.tensor_tensor(out=krp, in0=t1, in1=rstd, op=ALU.add),
                         reads=['kv_t1', 'kv_rstd'], writes=['kv_krp'])
                    S.dma('sp', kT_scr[g, :, 512 + tok0:512 + tok0 + 512], krp, reads=['kv_krp'], writes=[('kTs', g)])
                WS.add(win(L, AK0 + g * 128), 32, fk)

                def fv(slot, g=g):
                    if cfg.get('kvl', 9) < 4:
                        return
                    b = tbank()
                    tm_mm(slot, 32, hT, HK, [0, 128, 256, 384], b)
                    S.op('dve', lambda e: e.tensor_copy(out=vst, in_=PS(b)), reads=[('ps', b)], writes=['kv_vst'])
                    S.op('act', lambda e: e.activation(out=vbf, in_=PS(b), func=AF.Copy), reads=[('ps', b)], writes=['kv_vbf'])
                    for j in range(4):
                        sg = segs[j // 2]
                        S.dma('sp', cv_out[sg, L, (j % 2) * 128:(j % 2 + 1) * 128, g * 128:(g + 1) * 128],
                              vst[:, j * 128:(j + 1) * 128], reads=['kv_vst'], writes=[('cv', sg, g, j)])
                    S.dma('sp', v_scr[512 + tok0:512 + tok0 + 512, g * 128:(g + 1) * 128].rearrange("(t p) d -> p t d", p=128),
                          vbf.rearrange("p (t d) -> p t d", d=128), reads=['kv_vbf'], writes=[('vs', g)])
                WS.add(win(L, AV0 + g * 128), 32, fv)
            WS.run()

        def cache_step():
            AR.reset()
            ckf = AR.get(512)
            ckb = AR.get(256, BF16)
            cvf = AR.get(2048)
            cvb = AR.get(1024, BF16)
            for g in range(4):
                S.dma('sp', ckf, cache_k[L, g, :, :], writes=['ckf'])
                S.op('dve', lambda e: e.tensor_copy(out=ckb, in_=ckf), reads=['ckf'], writes=['ckb'])
                S.dma('sp', kT_scr[g, :, 0:512], ckb, reads=['ckb'], writes=[('kTs', g)])
            S.dma('sp', cvf.rearrange("p (t c) -> p t c", c=512), cache_v[L].rearrange("(t p) c -> p t c", p=128), writes=['cvf'])
            S.op('dve', lambda e: e.tensor_copy(out=cvb, in_=cvf), reads=['cvf'], writes=['cvb'])
            S.dma('sp', v_scr[0:512, :].rearrange("(t p) c -> p t c", p=128), cvb.rearrange("p (t c) -> p t c", c=512),
                  reads=['cvb'], writes=[('vs', g) for g in range(4)])

        def branch_attn(P):
            AR.reset()
            raw = AR.get(512)
            sq = AR.get(256, BF16)
            rstd = AR.get(512)
            knb = AR.get(256, BF16)
            t1 = AR.get(512)
            qT = AR.get(1024, BF16, (128, 4, 512))
            gat = AR.get(1024, BF16, (128, 4, 512))
            nkt = P['nkt']
            kT = AR.get(nkt * 64, BF16)
            vv = AR.get(nkt * 64, BF16, (128, nkt, 128))
            PTr = [AR.get(256, BF16) for _ in range(3)]
            rec = AR.get(512)
            tmp = AR.get(512)
            tok0 = P['tok0']
            krow0 = P['krow0']
            for g in range(4):
                for j in range(4):
                    hd = 4 * g + j

                    def fq(slot, j=j, hd=hd):
                        b = tbank()
                        fm_mm(slot, 32, hT, HK, (0, 512), b)
                        S.op('act', lambda e: e.activation(out=sq, in_=PS(b), func=AF.Square), reads=[('ps', b)], writes=['a_sq'])
                        S.op('dve', lambda e: e.tensor_copy(out=raw, in_=PS(b)), reads=[('ps', b)], writes=['a_raw'])
                        b2 = tbank()
                        S.op('pe', lambda e: e.matmul(PS(b2), cbf(C_ONE), sq, start=True, stop=True),
                             reads=['a_sq', 'cstb'], writes=[('ps', b2)])
                        rsqrt_(rstd, PS(b2), 1.0 / 128, [('ps', b2)], ['a_rstd'])
                        S.op('dve', lambda e: e.scalar_tensor_tensor(out=knb, in0=raw, scalar=qw[:, 0:1], in1=rstd,
                                                                     op0=ALU.mult, op1=ALU.mult),
                             reads=['a_raw', 'qw', 'a_rstd'], writes=['a_knb'])
                        b3 = tbank()
                        S.op('pe', lambda e: e.matmul(PS(b3), cbf(C_RM), knb, start=True, stop=True),
                             reads=['a_knb', 'cstb'], writes=[('ps', b3)])
                        S.op('dve', lambda e: e.tensor_tensor(out=t1, in0=knb, in1=ropec_t[:, tok0:tok0 + 512], op=ALU.mult),
                             reads=['a_knb', 'rope'], writes=['a_t1'])
                        S.op('dve', lambda e: e.tensor_tensor(out=rstd, in0=PS(b3), in1=ropes_t[:, tok0:tok0 + 512], op=ALU.mult),
                             reads=[('ps', b3), 'rope'], writes=['a_rstd'])
                        S.op('dve', lambda e: e.tensor_tensor(out=qT[:, j, :], in0=t1, in1=rstd, op=ALU.add),
                             reads=['a_t1', 'a_rstd'], writes=[('a_qT', j)])
                    WS.add(win(L, AQ0 + hd * 128), 32, fq)

                    def fg(slot, j=j, hd=hd, g=g):
                        b = tbank()
                        fm_mm(slot, 32, hT, HK, (0, 512), b)
                        S.op('act', lambda e: e.activation(out=gat[:, j, :], in_=PS(b), func=AF.Silu),
                             reads=[('ps', b)], writes=[('a_gat', j)])
                        if j == 3:
                            attn_group(g)
                    WS.add(win(L, AG0 + hd * 128), 32, fg)

            def attn_group(g):
                S.dma('sp', kT, kT_scr[g, :, krow0:krow0 + nkt * 128], reads=[('kTs', g)], writes=['a_kT'])
                S.dma('sp', vv, v_scr[krow0:krow0 + nkt * 128, g * 128:(g + 1) * 128].rearrange("(t p) d -> p t d", p=128),
                      reads=[('vs', g)], writes=['a_vv'])
                for j in range(4):
                    hd = 4 * g + j
                    ob, sbk = (4, 5) if (hd % 2 == 0) else (6, 7)
                    scb = {}

                    def emit_sc(kt, j=j):
                        b = tbank()
                        scb[kt] = b
                        S.op('pe', lambda e: e.matmul(PS(b), kT[:, kt * 128:(kt + 1) * 128], qT[:, j, :], start=True, stop=True),
                             reads=['a_kT', ('a_qT', j)], writes=[('ps', b)])

                    def emit_pv(kt):
                        b = scb[kt]
                        pt = PTr[kt % 3]
                        pk = ('a_pt', kt % 3)
                        for s in range(2):
                            bias = P['bias'](kt, s)
                            S.op('act', lambda e, s=s, bias=bias: e.activation(
                                out=pt[:, s * 256:(s + 1) * 256], in_=PS(b)[:, s * 256:(s + 1) * 256], func=AF.Exp,
                                bias=bias, scale=128 ** -0.5), reads=[('ps', b), 'abg', 'abs'], writes=[pk])
                        S.op('pe', lambda e: e.matmul(PS(ob), vv[:, kt, :], pt, start=(kt == 0), stop=(kt == nkt - 1)),
                             reads=['a_vv', pk], writes=[('ps', ob)])
                        S.op('pe', lambda e: e.matmul(PS(sbk), cbf(C_ONE), pt, start=(kt == 0), stop=(kt == nkt - 1)),
                             reads=['cstb', pk], writes=[('ps', sbk)])
                    emit_sc(0)
                    for kt in range(nkt):
                        if kt + 1 < nkt:
                            emit_sc(kt + 1)
                        emit_pv(kt)
                    S.op('dve', lambda e: e.reciprocal(out=rec, in_=PS(sbk)), reads=[('ps', sbk)], writes=['a_rec'])
                    S.op('dve', lambda e: e.tensor_tensor(out=tmp, in0=PS(ob), in1=rec, op=ALU.mult),
                         reads=[('ps', ob), 'a_rec'], writes=['a_tmp'])
                    S.op('dve', lambda e, j=j, hd=hd: e.tensor_tensor(out=oT[:, hd, :], in0=tmp, in1=gat[:, j, :], op=ALU.mult),
                         reads=['a_tmp', ('a_gat', j)], writes=['oT'])
            WS.run()

        def init_state(Sst, key, kind, src, link):
            if kind == 'zero':
                S.op('dve', lambda e: e.memset(Sst, 0.0), writes=[key])
            else:
                S.dma('sp', Sst, src, reads=[('scr', str(src.tensor.name) if hasattr(src, 'tensor') else 'x')], writes=[key])
                if link is not None:
                    S.op('dve', lambda e: e.tensor_scalar(out=Sst, in0=Sst, scalar1=link, scalar2=None, op0=ALU.mult),
                         reads=[key, 'flg'], writes=[key])

        def branch_ssd(P, mode):
            pre = (mode == 'pre')
            pi = P['pi']
            ext = P['ext'] is not None
            ea = P['ext_after']
            ecol = 0 if ea else 127
            linkmid = flg[:, 0:1] if P['group'] else flg[:, 1:2]
            AR.reset()
            raw = AR.get(512)
            rawe = AR.get(2)
            acc = AR.get(512)
            xcT = AR.get(1024, BF16, (128, 4, 512))
            BcT = AR.get(256, BF16)
            CcT = AR.get(256, BF16)
            xtm = AR.get(1024, BF16, (128, 4, 512))
            btm = AR.get(256, BF16, (128, 4, 128))
            zs = AR.get(1024, BF16, (128, 4, 512))
            dtt = AR.get(64, F32, (128, 4, 16))
            dta = AR.get(64, F32, (128, 4, 16))
            dtbg = AR.get(16)
            ag = AR.get(16)
            xpf = AR.get(1024, BF16, (128, 4, 512))
            xpb = AR.get(256, BF16)
            xpp = AR.get(256, BF16)
            Sst = [AR.get(512), AR.get(512)]
            Sfbf = AR.get(1024, BF16, (128, 4, 512))
            Sbbf = AR.get(256, BF16)
            Br = [AR.get(512, BF16, (128, 8, 128)), AR.get(512, BF16, (128, 8, 128))]
            Ew = [AR.get(512, BF16, (128, 8, 128)), AR.get(512, BF16, (128, 8, 128))]
            Ea = [AR.get(512, BF16, (128, 8, 128)), AR.get(512, BF16, (128, 8, 128))]
            CBm = [AR.get(64, BF16), AR.get(64, BF16)]
            dec = AR.get(16)
            tS = acc
            t1 = AR.get(512)
            t2 = AR.get(512)
            ybf = AR.get(256, BF16)
            snw = AR.get(512)
            stS = [AR.get(512), AR.get(512)]
            sto = {'i': 0}

            def out_state(Ssrc, skey, dst):
                st = stS[sto['i'] % 2]
                sk = ('b_stS', sto['i'] % 2)
                sto['i'] += 1
                S.op('act', lambda e: e.activation(out=st, in_=Ssrc, func=AF.Copy), reads=[skey], writes=[sk])
                S.dma('sp', dst, st, reads=[sk], writes=[('sso', str(dst.offset))])

            def conv_block(slot, cbk, dst, dkey, need_ext=True):
                b = tbank()
                fm_mm(slot, 32, hT, HK, (0, 512), b)
                S.op('dve', lambda e: e.tensor_copy(out=raw, in_=PS(b)), reads=[('ps', b)], writes=['b_raw'])
                if ext and need_ext:
                    b2 = tbank()
                    fm_mm(slot, 32, hTe, HEK, (ecol, 1), b2)
                    S.op('dve', lambda e: e.tensor_copy(out=rawe[:, 0:1], in_=PS(b2)[:, 0:1]), reads=[('ps', b2)], writes=['b_rawe'])
                S.op('dve', lambda e: e.tensor_scalar(out=acc, in0=raw, scalar1=cw[:, cbk, 1:2], scalar2=cb[:, cbk:cbk + 1],
                                                      op0=ALU.mult, op1=ALU.add), reads=['b_raw', 'cw', 'cb'], writes=['b_acc'])
                r3 = raw.rearrange("p (s t) -> p s t", t=256)
                a3 = acc.rearrange("p (s t) -> p s t", t=256)
                S.op('dve', lambda e: e.scalar_tensor_tensor(out=a3[:, :, 1:256], in0=r3[:, :, 0:255], scalar=cw[:, cbk, 0:1],
                                                             in1=a3[:, :, 1:256], op0=ALU.mult, op1=ALU.add),
                     reads=['b_raw', 'b_acc', 'cw'], writes=['b_acc'])
                S.op('dve', lambda e: e.scalar_tensor_tensor(out=a3[:, :, 0:255], in0=r3[:, :, 1:256], scalar=cw[:, cbk, 2:3],
                                                             in1=a3[:, :, 0:255], op0=ALU.mult, op1=ALU.add),
                     reads=['b_raw', 'b_acc', 'cw'], writes=['b_acc'])
                if P['group']:
                    S.op('dve', lambda e: e.scalar_tensor_tensor(out=acc[:, 256:257], in0=raw[:, 255:256], scalar=cwl[:, cbk, 0:1],
                                                                 in1=acc[:, 256:257], op0=ALU.mult, op1=ALU.add),
                         reads=['b_raw', 'b_acc', 'cwl'], writes=['b_acc'])
                    S.op('dve', lambda e: e.scalar_tensor_tensor(out=acc[:, 255:256], in0=raw[:, 256:257], scalar=cwl[:, cbk, 1:2],
                                                                 in1=acc[:, 255:256], op0=ALU.mult, op1=ALU.add),
                         reads=['b_raw', 'b_acc', 'cwl'], writes=['b_acc'])
                if ext and need_ext:
                    if ea:
                        S.op('dve', lambda e: e.scalar_tensor_tensor(out=acc[:, 511:512], in0=rawe[:, 0:1], scalar=cwl[:, cbk, 1:2],
                                                                     in1=acc[:, 511:512], op0=ALU.mult, op1=ALU.add),
                             reads=['b_rawe', 'b_acc', 'cwl'], writes=['b_acc'])
                    else:
                        S.op('dve', lambda e: e.scalar_tensor_tensor(out=acc[:, 0:1], in0=rawe[:, 0:1], scalar=cwl[:, cbk, 0:1],
                                                                     in1=acc[:, 0:1], op0=ALU.mult, op1=ALU.add),
                             reads=['b_rawe', 'b_acc', 'cwl'], writes=['b_acc'])
                S.op('act', lambda e: e.activation(out=dst, in_=acc, func=AF.Silu), reads=['b_acc'], writes=[dkey])

            def state_update(di, t, g):
                fwd = (di == 0)
                dc = 0 if fwd else 8
                xp = xpf[:, t, :] if fwd else xpb
                xk = ('b_xpf', t) if fwd else 'b_xpb'
                b = tbank()

                def f(e):
                    e.matmul(PS(b)[:, 0:8], cf(C_SLF if fwd else C_SUB), dta[:, t, dc:dc + 8], start=True, stop=True)
                    return e.matmul(PS(b)[:, 8:16], cf(C_ONE), dta[:, t, dc:dc + 8], start=True, stop=True)
                S.op('pe', f, reads=['cstf', 'b_dta'], writes=[('ps', b)])
                S.op('act', lambda e: e.activation(out=dec, in_=PS(b)[:, 0:16], func=AF.Exp), reads=[('ps', b)], writes=['b_dec'])
                S.op('dve', lambda e: e.tensor_tensor(out=xpp.rearrange("p (h q) -> p h q", q=64),
                                                      in0=xp.rearrange("p (h q) -> p h q", q=64),
                                                      in1=dec[:, 0:8].unsqueeze(2).broadcast_to([128, 8, 64]), op=ALU.mult),
                     reads=[xk, 'b_dec'], writes=['b_xpp'])
                b2 = tbank()
                S.op('pe', lambda e: e.matmul(PS(b2), btm[:, t, :], xpp, start=True, stop=True),
                     reads=['b_btm', 'b_xpp'], writes=[('ps', b2)])
                sk = ('b_S', di)
                S.op('dve', lambda e: e.tensor_tensor(out=tS.rearrange("p (h q) -> p h q", q=64),
                                                      in0=Sst[di].rearrange("p (h q) -> p h q", q=64),
                                                      in1=dec[:, 8:16].unsqueeze(2).broadcast_to([128, 8, 64]), op=ALU.mult),
                     reads=[sk, 'b_dec'], writes=['b_tS'])
                S.op('dve', lambda e: e.tensor_tensor(out=Sst[di], in0=tS, in1=PS(b2), op=ALU.add),
                     reads=['b_tS', ('ps', b2)], writes=[sk])

            def ssd_group(g):
                gseg = P['segs']
                for t in range(4):
                    b = tbank()

                    def f(e, t=t, b=b):
                        ins = None
                        for j in range(4):
                            ins = e.transpose(PSB(b)[:, j * 128:(j + 1) * 128], xcT[:, j, t * 128:(t + 1) * 128], cbf(C_ID))
                        return e.transpose(PSB(b)[:, 512:640], BcT[:, t * 128:(t + 1) * 128], cbf(C_ID))
                    S.op('pe', f, reads=[('b_xcT', j) for j in range(4)] + ['b_BcT', 'cstb'], writes=[('ps', b)])
                    S.op('dve', lambda e, t=t, b=b: e.tensor_copy(out=xtm[:, t, :], in_=PSB(b)[:, 0:512]), reads=[('ps', b)], writes=['b_xtm'])
                    S.op('act', lambda e, t=t, b=b: e.activation(out=btm[:, t, :], in_=PSB(b)[:, 512:640], func=AF.Copy),
                         reads=[('ps', b)], writes=['b_btm'])
                if pre:
                    init_state(Sst[1], ('b_S', 1), 'dram', st_ssm[L, 1, g], None)
                else:
                    if pi == 0:
                        init_state(Sst[0], ('b_S', 0), 'dram', st_ssm[L, 0, g], None)
                        init_state(Sst[1], ('b_S', 1), 'dram', sbp_ssm[g], flg[:, 0:1])
                    elif pi == 1:
                        init_state(Sst[0], ('b_S', 0), 'dram', sfc_ssm[g], flg[:, 0:1])
                        init_state(Sst[1], ('b_S', 1), 'dram', st_ssm[L, 1, g], None)
                    else:
                        init_state(Sst[0], ('b_S', 0), 'zero', None, None)
                        init_state(Sst[1], ('b_S', 1), 'zero', None, None)
                if not pre:
                    for t in range(4):
                        S.op('dve', lambda e, t=t: e.tensor_tensor(out=xpf[:, t, :].rearrange("p (h q) -> p h q", q=64),
                                                                   in0=xtm[:, t, :].rearrange("p (h q) -> p h q", q=64),
                                                                   in1=dtt[:, t, 0:8].unsqueeze(2).broadcast_to([128, 8, 64]), op=ALU.mult),
                             reads=['b_xtm', 'b_dtt'], writes=[('b_xpf', t)])
                        if t == 2:
                            out_state(Sst[0], ('b_S', 0), ss_out[gseg[0], L, 0, g])
                            S.op('dve', lambda e: e.tensor_scalar(out=Sst[0], in0=Sst[0], scalar1=linkmid, scalar2=None, op0=ALU.mult),
                                 reads=[('b_S', 0), 'flg'], writes=[('b_S', 0)])
                        S.op('act', lambda e, t=t: e.activation(out=Sfbf[:, t, :], in_=Sst[0], func=AF.Copy),
                             reads=[('b_S', 0)], writes=[('b_Sfbf', t)])
                        state_update(0, t, g)
                    out_state(Sst[0], ('b_S', 0), ss_out[gseg[1], L, 0, g])
                    if pi == 0:
                        out_state(Sst[0], ('b_S', 0), sfc_ssm[g])
                for t in (3, 2, 1, 0):
                    if t == 1:
                        if not pre:
                            out_state(Sst[1], ('b_S', 1), ss_out[gseg[1], L, 1, g])
                        S.op('dve', lambda e: e.tensor_scalar(out=Sst[1], in0=Sst[1], scalar1=linkmid, scalar2=None, op0=ALU.mult),
                             reads=[('b_S', 1), 'flg'], writes=[('b_S', 1)])
                    S.op('dve', lambda e, t=t: e.tensor_tensor(out=xpb.rearrange("p (h q) -> p h q", q=64),
                                                               in0=xtm[:, t, :].rearrange("p (h q) -> p h q", q=64),
                                                               in1=dtt[:, t, 8:16].unsqueeze(2).broadcast_to([128, 8, 64]), op=ALU.mult),
                         reads=['b_xtm', 'b_dtt'], writes=['b_xpb'])
                    if not pre:
                        S.op('act', lambda e: e.activation(out=Sbbf, in_=Sst[1], func=AF.Copy), reads=[('b_S', 1)], writes=['b_Sbbf'])
                        tc_ = slice(t * 128, (t + 1) * 128)
                        bc_ = tbank()
                        S.op('pe', lambda e: e.matmul(PS(bc_)[:, 0:128], BcT[:, tc_], CcT[:, tc_], start=True, stop=True),
                             reads=['b_BcT', 'b_CcT'], writes=[('ps', bc_)])
                        S.op('dve', lambda e: e.tensor_tensor(out=CBm[0], in0=PS(bc_)[:, 0:128], in1=cbf(C_MF), op=ALU.mult),
                             reads=[('ps', bc_), 'cstb'], writes=[('b_CBm', 0)])
                        S.op('dve', lambda e: e.tensor_tensor(out=CBm[1], in0=PS(bc_)[:, 0:128], in1=cbf(C_MB), op=ALU.mult),
                             reads=[('ps', bc_), 'cstb'], writes=[('b_CBm', 1)])
                        for di in range(2):
                            dc = 0 if di == 0 else 8
                            Mm = C_MF if di == 0 else C_MB
                            Lm = C_SLF if di == 0 else C_SUB
                            S.op('dve', lambda e, di=di, dc=dc, Mm=Mm: e.tensor_tensor(
                                out=Br[di], in0=cbf(Mm).unsqueeze(1).broadcast_to([128, 8, 128]),
                                in1=dta[:, t, dc:dc + 8].unsqueeze(2).broadcast_to([128, 8, 128]), op=ALU.mult),
                                reads=['cstb', 'b_dta'], writes=[('b_Br', di)])
                            brf = Br[di].rearrange("p h l -> p (h l)")
                            ewf = Ew[di].rearrange("p h l -> p (h l)")
                            eaf = Ea[di].rearrange("p h l -> p (h l)")
                            for hf in range(2):
                                b0 = tbank()
                                S.op('pe', lambda e, b0=b0, hf=hf, Lm=Lm: e.matmul(PS(b0), cbf(Lm), brf[:, hf * 512:(hf + 1) * 512], start=True, stop=True),
                                     reads=['cstb', ('b_Br', di)], writes=[('ps', b0)])
                                S.op('act', lambda e, b0=b0, hf=hf: e.activation(out=ewf[:, hf * 512:(hf + 1) * 512], in_=PS(b0), func=AF.Exp),
                                     reads=[('ps', b0)], writes=[('b_Ew', di)])
                            for hf in range(2):
                                b0 = tbank()
                                S.op('pe', lambda e, b0=b0, hf=hf: e.matmul(PS(b0), cbf(C_ONE), brf[:, hf * 512:(hf + 1) * 512], start=True, stop=True),
                                     reads=['cstb', ('b_Br', di)], writes=[('ps', b0)])
                                S.op('act', lambda e, b0=b0, hf=hf: e.activation(out=eaf[:, hf * 512:(hf + 1) * 512], in_=PS(b0), func=AF.Exp),
                                     reads=[('ps', b0)], writes=[('b_Ea', di)])
                            S.op('dve', lambda e, di=di: e.tensor_tensor(out=Ew[di], in0=Ew[di], in1=CBm[di].unsqueeze(1).broadcast_to([128, 8, 128]), op=ALU.mult),
                                 reads=[('b_Ew', di), ('b_CBm', di)], writes=[('b_Ew', di)])
                            S.op('dve', lambda e, di=di: e.tensor_tensor(out=Ea[di], in0=Ea[di], in1=CcT[:, tc_].unsqueeze(1).broadcast_to([128, 8, 128]), op=ALU.mult),
                                 reads=[('b_Ea', di), 'b_CcT'], writes=[('b_Ea', di)])
                        yb = 4 + (t % 2)

                        def fy(e, t=t, yb=yb):
                            ins = None
                            for h in range(8):
                                hs = slice(h * 64, (h + 1) * 64)
                                e.matmul(PS(yb)[:, hs], Ew[0][:, h, :], xpf[:, t, hs], start=True, stop=False)
                                e.matmul(PS(yb)[:, hs], Ea[0][:, h, :], Sfbf[:, t, hs], start=False, stop=False)
                                e.matmul(PS(yb)[:, hs], Ew[1][:, h, :], xpb[:, hs], start=False, stop=False)
                                ins = e.matmul(PS(yb)[:, hs], Ea[1][:, h, :], Sbbf[:, hs], start=False, stop=True)
                            return ins
                        S.op('pe', fy, reads=[('b_Ew', 0), ('b_Ew', 1), ('b_Ea', 0), ('b_Ea', 1), ('b_xpf', t), 'b_xpb', ('b_Sfbf', t), 'b_Sbbf'],
                             writes=[('ps', yb)])
                        S.op('dve', lambda e, t=t: e.tensor_tensor(out=t1.rearrange("p (h q) -> p h q", q=64),
                                                                   in0=xtm[:, t, :].rearrange("p (h q) -> p h q", q=64),
                                                                   in1=dsk_b[:, 8 * g:8 * g + 8].unsqueeze(2).broadcast_to([128, 8, 64]), op=ALU.mult),
                             reads=['b_xtm', 'dsk'], writes=['b_t1'])
                        S.op('dve', lambda e, yb=yb: e.tensor_tensor(out=t1, in0=t1, in1=PS(yb), op=ALU.add),
                             reads=['b_t1', ('ps', yb)], writes=['b_t1'])
                        S.op('dve', lambda e, t=t: e.tensor_tensor(out=t2, in0=t1, in1=zs[:, t, :], op=ALU.mult),
                             reads=['b_t1', 'b_zs'], writes=['b_t2'])
                        ssq = small[:, 2:3]
                        S.op('act', lambda e: e.activation(out=t1, in_=t2, func=AF.Square, accum_out=ssq), reads=['b_t2'], writes=['b_t1', 'b_ssq'])
                        rsqrt_(ssq, ssq, 1.0 / 512, ['b_ssq'], ['b_ssq'])
                        S.op('dve', lambda e: e.scalar_tensor_tensor(out=ybf, in0=t2, scalar=ssq, in1=snw, op0=ALU.mult, op1=ALU.mult),
                             reads=['b_t2', 'b_ssq', 'b_snw'], writes=['b_ybf'])
                        bt = tbank()

                        def ft(e, bt=bt):
                            ins = None
                            for j in range(4):
                                ins = e.transpose(PSB(bt)[:, j * 128:(j + 1) * 128], ybf[:, j * 128:(j + 1) * 128], cbf(C_ID))
                            return ins
                        S.op('pe', ft, reads=['b_ybf', 'cstb'], writes=[('ps', bt)])
                        S.op('act', lambda e, t=t, bt=bt: e.activation(out=oT[:, 4 * g:4 * g + 4, t * 128:(t + 1) * 128],
                                                                     in_=PSB(bt)[:, 0:512].rearrange("p (j c) -> p j c", c=128), func=AF.Copy),
                             reads=[('ps', bt)], writes=['oT'])
                    state_update(1, t, g)
                if pre:
                    out_state(Sst[1], ('b_S', 1), sbp_ssm[g])
                else:
                    out_state(Sst[1], ('b_S', 1), ss_out[gseg[0], L, 1, g])

            for g in range(4):
                def f_snw(slot, g=g):
                    pass
                for j in range(4):
                    cbk = 4 * g + j
                    WS.add(win(L, BX0 + cbk * 128), 32,
                           (lambda slot, cbk=cbk, j=j: conv_block(slot, cbk, xcT[:, j, :], ('b_xcT', j))))
                WS.add(win(L, BB0 + g * 128), 32,
                       (lambda slot, g=g: conv_block(slot, 16 + g, BcT, 'b_BcT')))
                if not pre:
                    WS.add(win(L, BC0 + g * 128), 32,
                           (lambda slot, g=g: conv_block(slot, 20 + g, CcT, 'b_CcT')))
                    for j in range(4):
                        def fz(slot, g=g, j=j):
                            b = tbank()
                            tm_mm(slot, 32, hT, HK, [0, 128, 256, 384], b)
                            S.op('act', lambda e: e.activation(out=zs[:, :, j * 128:(j + 1) * 128],
                                                               in_=PS(b).rearrange("p (t c) -> p t c", c=128), func=AF.Silu),
                                 reads=[('ps', b)], writes=['b_zs'])
                        c0 = BZ0 + (4 * g + j) * 128
                        WS.add(win(L, c0), 32, fz)

                def fdt(slot, g=g):
                    b = tbank()
                    tm_mm(slot, 32, hT, HK, [0, 128, 256, 384], b, ncolw=16)
                    S.op('dve', lambda e: e.tensor_copy(out=dtbg[:, 0:8], in_=dtb_b[:, 8 * g:8 * g + 8]), reads=['dtb'], writes=['b_dtbg'])
                    S.op('dve', lambda e: e.tensor_copy(out=dtbg[:, 8:16], in_=dtb_b[:, 32 + 8 * g:40 + 8 * g]), reads=['dtb'], writes=['b_dtbg'])
                    S.op('dve', lambda e: e.tensor_copy(out=ag[:, 0:8], in_=a_b[:, 8 * g:8 * g + 8]), reads=['a_b'], writes=['b_ag'])
                    S.op('dve', lambda e: e.tensor_copy(out=ag[:, 8:16], in_=a_b[:, 32 + 8 * g:40 + 8 * g]), reads=['a_b'], writes=['b_ag'])
                    S.op('dve', lambda e: e.tensor_tensor(out=dtt, in0=PS(b)[:, 0:64].rearrange("p (t c) -> p t c", c=16),
                                                          in1=dtbg.unsqueeze(1).broadcast_to([128, 4, 16]), op=ALU.add),
                         reads=[('ps', b), 'b_dtbg'], writes=['b_dtt'])
                    S.op('act', lambda e: e.activation(out=dtt, in_=dtt, func=AF.Exp), reads=['b_dtt'], writes=['b_dtt'])
                    S.op('act', lambda e: e.activation(out=dtt, in_=dtt, func=AF.Ln, bias=cc[:, 1:2]), reads=['b_dtt'], writes=['b_dtt'])
                    S.op('dve', lambda e: e.tensor_tensor(out=dta, in0=dtt, in1=ag.unsqueeze(1).broadcast_to([128, 4, 16]), op=ALU.mult),
                         reads=['b_dtt', 'b_ag'], writes=['b_dta'])
                    if not pre:
                        S.dma('sp', snw, ssm_norm_w[L][g * 512:(g + 1) * 512].partition_broadcast(128), writes=['b_snw'])
                    ssd_group(g)
                WS.add([(w_dt[L][:, 8 * g:8 * g + 8], 0), (w_dt[L][:, 32 + 8 * g:40 + 8 * g], 8)], 32, fdt, ncol=16)
            WS.run()

        def branch_hgrn(P, mode):
            pre = (mode == 'pre')
            pi = P['pi']
            linkmid = flg[:, 0:1] if P['group'] else flg[:, 1:2]
            AR.reset()
            a1 = AR.get(512)
            a2 = AR.get(512)
            cum = AR.get(512)
            Pp = AR.get(512)
            Dd = AR.get(512)
            eX = AR.get(512)
            eP = AR.get(512)
            kk = AR.get(512)
            qs = AR.get(512)
            qt = AR.get(256, BF16)
            kt = AR.get(256, BF16)
            qh = AR.get(256, BF16)
            khT = AR.get(256, BF16)
            khtm = AR.get(256, BF16, (128, 4, 128))
            vtm = AR.get(256, BF16, (128, 4, 128))
            gs = AR.get(256, BF16)
            attm = AR.get(64, BF16)
            attf = AR.get(128)
            Sst = [AR.get(128), AR.get(128)]
            Sbf = AR.get(64, BF16)
            of = AR.get(512)
            osum = AR.get(512)
            sq = AR.get(256, BF16)
            rstd = AR.get(512)
            stS = [AR.get(128), AR.get(128)]
            sto = {'i': 0}
            QS = 128 ** -0.5

            def out_state(Ssrc, skey, dst):
                st = stS[sto['i'] % 2]
                sk = ('c_stS', sto['i'] % 2)
                sto['i'] += 1
                S.op('act', lambda e: e.activation(out=st, in_=Ssrc, func=AF.Copy), reads=[skey], writes=[sk])
                S.dma('sp', dst, st, reads=[sk], writes=[('sho', str(dst.offset))])

            def v3(a):
                return a.rearrange("p (c t) -> p c t", t=64)

            def dir_prep(b, di, h):
                lbc = lbT[:, L, di, h:h + 1]
                S.op('act', lambda e: e.activation(out=eX, in_=PS(b), func=AF.Exp, scale=-1.0), reads=[('ps', b)], writes=['c_eX'])
                S.op('act', lambda e: e.activation(out=a1, in_=eX, func=AF.Ln, bias=cc[:, 1:2]), reads=['c_eX'], writes=['c_a1'])
                S.op('act', lambda e: e.activation(out=a2, in_=eX, func=AF.Ln, bias=cc[:, 1:2], scale=lbc), reads=['c_eX', 'lbT'], writes=['c_a2'])
                S.op('dve', lambda e: e.tensor_tensor(out=a2, in0=a2, in1=a1, op=ALU.subtract), reads=['c_a1', 'c_a2'], writes=['c_a2'])
                S.op('act', lambda e: e.activation(out=a1, in_=a1, func=AF.Exp, scale=-1.0), reads=['c_a1'], writes=['c_a1'])
                S.op('dve', lambda e: e.tensor_scalar(out=kk, in0=a1, scalar1=nomlT[:, L, di, h:h + 1], scalar2=omlT[:, L, di, h:h + 1],
                                                      op0=ALU.mult, op1=ALU.add), reads=['c_a1', 'omlT', 'nomlT'], writes=['c_kk'])
                S.op('dve', lambda e: e.tensor_tensor_scan(out=cum, data0=cf(C_RST, 512), data1=a2, initial=0.0, op0=ALU.mult, op1=ALU.add),
                     reads=['cstf', 'c_a2'], writes=['c_cum'])
                if di == 0:
                    S.op('dve', lambda e: e.tensor_copy(out=Pp, in_=cum), reads=['c_cum'], writes=['c_Pp'])
                    mid, ti = 31, 63
                else:
                    S.op('dve', lambda e: e.tensor_tensor(out=Dd, in0=a2, in1=cum, op=ALU.subtract), reads=['c_a2', 'c_cum'], writes=['c_Dd'])
                    S.op('dve', lambda e: e.tensor_tensor(out=v3(Pp), in0=v3(Dd), in1=v3(cum)[:, :, 63:64].broadcast_to([128, 8, 64]), op=ALU.add),
                         reads=['c_Dd', 'c_cum'], writes=['c_Pp'])
                    mid, ti = 32, 0
                if not pre:
                    S.op('dve', lambda e: e.tensor_tensor(out=v3(Dd), in0=v3(Pp), in1=v3(Pp)[:, :, mid:mid + 1].broadcast_to([128, 8, 64]), op=ALU.subtract),
                         reads=['c_Pp'], writes=['c_Dd'])
                    S.op('act', lambda e: e.activation(out=eX, in_=Dd, func=AF.Exp), reads=['c_Dd'], writes=['c_eX'])
                    S.op('dve', lambda e: e.scalar_tensor_tensor(out=qt, in0=qs, scalar=QS, in1=eX, op0=ALU.mult, op1=ALU.mult),
                         reads=['c_qs', 'c_eX'], writes=['c_qt'])
                    S.op('act', lambda e: e.activation(out=eX, in_=Dd, func=AF.Exp, scale=-1.0), reads=['c_Dd'], writes=['c_eX'])
                    S.op('dve', lambda e: e.tensor_tensor(out=kt, in0=kk, in1=eX, op=ALU.mult), reads=['c_kk', 'c_eX'], writes=['c_kt'])
                S.op('act', lambda e: e.activation(out=eP, in_=Pp, func=AF.Exp), reads=['c_Pp'], writes=['c_eP'])
                if not pre:
                    S.op('dve', lambda e: e.scalar_tensor_tensor(out=qh, in0=qs, scalar=QS, in1=eP, op0=ALU.mult, op1=ALU.mult),
                         reads=['c_qs', 'c_eP'], writes=['c_qh'])
                S.op('dve', lambda e: e.tensor_tensor(out=v3(Dd), in0=v3(Pp)[:, :, ti:ti + 1].broadcast_to([128, 8, 64]), in1=v3(Pp), op=ALU.subtract),
                     reads=['c_Pp'], writes=['c_Dd'])
                S.op('act', lambda e: e.activation(out=eX, in_=Dd, func=AF.Exp), reads=['c_Dd'], writes=['c_eX'])
                S.op('dve', lambda e: e.tensor_tensor(out=khT, in0=kk, in1=eX, op=ALU.mult), reads=['c_kk', 'c_eX'], writes=['c_khT'])
                bt = tbank()

                def ft(e):
                    ins = None
                    for t in range(4):
                        ins = e.transpose(PSB(bt)[:, t * 128:(t + 1) * 128], khT[:, t * 128:(t + 1) * 128], cbf(C_ID))
                    return ins
                S.op('pe', ft, reads=['c_khT', 'cstb'], writes=[('ps', bt)])
                S.op('dve', lambda e: e.tensor_copy(out=khtm, in_=PSB(bt)[:, 0:512].rearrange("p (t c) -> p t c", c=128)),
                     reads=[('ps', bt)], writes=['c_khtm'])
                return ti

            def sweep(di, h, ti):
                fwd = (di == 0)
                gseg = P['segs']
                sk = ('c_S', di)
                if pre:
                    init_state(Sst[1], sk, 'dram', st_hg[L, 1, h], None)
                elif pi == 0:
                    if fwd:
                        init_state(Sst[0], sk, 'dram', st_hg[L, 0, h], None)
                    else:
                        init_state(Sst[1], sk, 'dram', sbp_hg[h], flg[:, 0:1])
                elif pi == 1:
                    if fwd:
                        init_state(Sst[0], sk, 'dram', sfc_hg[h], flg[:, 0:1])
                    else:
                        init_state(Sst[1], sk, 'dram', st_hg[L, 1, h], None)
                else:
                    init_state(Sst[di], sk, 'zero', None, None)
                St = Sst[di]
                if not pre:
                    S.op('act', lambda e: e.activation(out=Sbf, in_=St, func=AF.Copy), reads=[sk], writes=['c_Sbf'])
                tiles = (0, 1, 2, 3) if fwd else (3, 2, 1, 0)
                for t in tiles:
                    tcs = slice(t * 128, (t + 1) * 128)
                    ob = 4 + (t % 2) + (0 if fwd else 2)
                    if not pre:
                        b = tbank()
                        S.op('pe', lambda e, b=b, tcs=tcs: e.matmul(PS(b)[:, 0:128], kt[:, tcs], qt[:, tcs], start=True, stop=True),
                             reads=['c_kt', 'c_qt'], writes=[('ps', b)])
                        S.op('dve', lambda e, b=b: e.tensor_scalar(out=attf, in0=PS(b)[:, 0:128], scalar1=3.0e38, scalar2=-3.0e38,
                                                                  op0=ALU.min, op1=ALU.max),
                             reads=[('ps', b)], writes=['c_attf'])
                        S.op('dve', lambda e: e.tensor_tensor(out=attm, in0=attf, in1=cbf(C_BDF if fwd else C_BDB), op=ALU.mult),
                             reads=['c_attf', 'cstb'], writes=['c_attm'])
                        S.op('pe', lambda e, ob=ob, t=t: e.matmul(PS(ob)[:, 0:128], vtm[:, t, :], attm, start=True, stop=False),
                             reads=['c_vtm', 'c_attm'], writes=[('ps', ob)])
                    chunks = (2 * t, 2 * t + 1) if fwd else (2 * t + 1, 2 * t)
                    for ci, c in enumerate(chunks):
                        hf = c % 2
                        ccs = slice(c * 64, (c + 1) * 64)
                        ps_ = slice(hf * 64, (hf + 1) * 64)
                        if not pre:
                            S.op('pe', lambda e, ob=ob, ps_=ps_, ccs=ccs, ci=ci: e.matmul(PS(ob)[:, ps_], Sbf, qh[:, ccs], start=False, stop=(ci == 1)),
                                 reads=['c_Sbf', 'c_qh'], writes=[('ps', ob)])
                        b2 = tbank()
                        S.op('pe', lambda e, b2=b2, ps_=ps_, t=t: e.matmul(PS(b2)[:, 0:128], khtm[ps_, t, :], vtm[ps_, t, :], start=True, stop=True),
                             reads=['c_khtm', 'c_vtm'], writes=[('ps', b2)])
                        dcol = c * 64 + ti
                        S.op('dve', lambda e, b2=b2, dcol=dcol: e.scalar_tensor_tensor(out=St, in0=St, scalar=eP[:, dcol:dcol + 1], in1=PS(b2)[:, 0:128],
                                                                                      op0=ALU.mult, op1=ALU.add),
                             reads=[sk, 'c_eP', ('ps', b2)], writes=[sk])
                        seg_end = (c % 4 == 3) if fwd else (c % 4 == 0)
                        if seg_end:
                            sgi = gseg[c // 4]
                            if pre:
                                if c == 0:
                                    out_state(St, sk, sbp_hg[h])
                            else:
                                out_state(St, sk, sh_out[sgi, L, di, h])
                                if fwd and c == 7 and pi == 0:
                                    out_state(St, sk, sfc_hg[h])
                            if (fwd and c == 3) or ((not fwd) and c == 4):
                                S.op('dve', lambda e: e.tensor_scalar(out=St, in0=St, scalar1=linkmid, scalar2=None, op0=ALU.mult),
                                     reads=[sk, 'flg'], writes=[sk])
                        if not pre:
                            S.op('act', lambda e: e.activation(out=Sbf, in_=St, func=AF.Copy), reads=[sk], writes=['c_Sbf'])
                    if not pre:
                        if fwd:
                            S.op('act', lambda e, ob=ob, tcs=tcs: e.activation(out=of[:, tcs], in_=PS(ob)[:, 0:128], func=AF.Copy),
                                 reads=[('ps', ob)], writes=['c_of'])
                        else:
                            S.op('dve', lambda e, ob=ob, tcs=tcs: e.tensor_tensor(out=osum[:, tcs], in0=PS(ob)[:, 0:128], in1=of[:, tcs], op=ALU.add),
                                 reads=[('ps', ob), 'c_of'], writes=['c_osum'])

            for h in range(16):
                def fci(slot, h=h):
                    b = tbank()
                    tm_mm(slot, 32, hT, HK, [0, 128, 256, 384], b)
                    S.op('act', lambda e: e.activation(out=vtm, in_=PS(b).rearrange("p (t c) -> p t c", c=128), func=AF.Copy),
                         reads=[('ps', b)], writes=['c_vtm'])
                WS.add(win(L, CI0 + h * 128), 32, fci)
                if not pre:
                    def fcq(slot, h=h):
                        b = tbank()
                        fm_mm(slot, 32, hT, HK, (0, 512), b)
                        S.op('act', lambda e: e.activation(out=qs, in_=PS(b), func=AF.Silu), reads=[('ps', b)], writes=['c_qs'])
                    WS.add(win(L, CQ0 + h * 128), 32, fcq)

                    def fcg(slot, h=h):
                        b = tbank()
                        fm_mm(slot, 32, hT, HK, (0, 512), b)
                        S.op('act', lambda e: e.activation(out=gs, in_=PS(b), func=AF.Silu), reads=[('ps', b)], writes=['c_gs'])
                    WS.add(win(L, CG0 + h * 128), 32, fcg)

                    def fcf(slot, h=h):
                        b = tbank()
                        fm_mm(slot, 32, hT, HK, (0, 512), b)
                        ti = dir_prep(b, 0, h)
                        sweep(0, h, ti)
                    WS.add(win(L, CF0 + h * 128), 32, fcf)

                def fcb(slot, h=h):
                    b = tbank()
                    fm_mm(slot, 32, hT, HK, (0, 512), b)
                    ti = dir_prep(b, 1, h)
                    sweep(1, h, ti)
                    if not pre:
                        S.op('act', lambda e: e.activation(out=sq, in_=osum, func=AF.Square), reads=['c_osum'], writes=['c_sq'])
                        b2 = tbank()
                        S.op('pe', lambda e: e.matmul(PS(b2), cbf(C_ONE), sq, start=True, stop=True), reads=['c_sq', 'cstb'], writes=[('ps', b2)])
                        rsqrt_(rstd, PS(b2), 1.0 / 128, [('ps', b2)], ['c_rstd'])
                        S.op('dve', lambda e: e.scalar_tensor_tensor(out=rstd, in0=osum, scalar=hnw[:, h:h + 1], in1=rstd, op0=ALU.mult, op1=ALU.mult),
                             reads=['c_osum', 'hnw', 'c_rstd'], writes=['c_rstd'])
                        S.op('dve', lambda e: e.tensor_tensor(out=oT[:, h, :], in0=rstd, in1=gs, op=ALU.mult), reads=['c_rstd', 'c_gs'], writes=['oT'])
                WS.add(win(L, CF0 + 2048 + h * 128), 32, fcb)
            WS.run()

        def branch_merge(i):
            AR.reset()
            sg = AR.get(512)
            tmp = AR.get(512)
            for db in range(32):
                st = {}

                def fb(slot, db=db, st=st):
                    b = tbank()
                    st['b'] = b
                    fm_mm(slot, 16, oT, 'oT', (0, 512), b)
                WS.add(w_bout[L, i, db], 16, fb)

                def fgate(slot, db=db, st=st):
                    b2 = tbank()
                    fm_mm(slot, 32, hT, HK, (0, 512), b2)
                    S.op('act', lambda e: e.activation(out=sg, in_=PS(b2), func=AF.Sigmoid), reads=[('ps', b2)], writes=['m_sg'])
                    mk = ('mg', db // 16)
                    if i == branches[0]:
                        S.op('dve', lambda e: e.tensor_tensor(out=mgT[:, db, :], in0=PS(st['b']), in1=sg, op=ALU.mult),
                             reads=[('ps', st['b']), 'm_sg'], writes=[mk])
                    else:
                        S.op('dve', lambda e: e.tensor_tensor(out=tmp, in0=PS(st['b']), in1=sg, op=ALU.mult),
                             reads=[('ps', st['b']), 'm_sg'], writes=['m_tmp'])
                        S.op('dve', lambda e: e.tensor_tensor(out=mgT[:, db, :], in0=mgT[:, db, :], in1=tmp, op=ALU.add),
                             reads=['m_tmp', mk], writes=[mk])
                c0 = MG0 + i * D + db * 128
                WS.add(win(L, c0), 32, fgate)
            WS.run()

        def out_phase(tiles, mrow):
            AR.reset()
            og = AR.get(512)
            xr = [AR.get(512), AR.get(512)]
            yst = [AR.get(512), AR.get(512)]
            row0 = tiles[0] * 128
            for db in range(32):
                def fo(slot, db=db):
                    b = tbank()
                    fm_mm(slot, 32, mgT, None, (0, 512), b)
                    S.op('act', lambda e: e.activation(out=og, in_=PS(b), func=AF.Copy, scale=modT[:, 64 + db, mrow:mrow + 1]),
                         reads=[('ps', b), 'modT'], writes=['o_og'])
                    b2 = tbank()

                    def f(e):
                        ins = None
                        for t in range(4):
                            ins = e.transpose(PS(b2)[:, t * 128:(t + 1) * 128], og[:, t * 128:(t + 1) * 128], cf(C_ID))
                        return ins
                    S.op('pe', f, reads=['o_og', 'cstf'], writes=[('ps', b2)])
                    xrt = xr[db % 2]
                    S.dma('sp', xrt.rearrange("p (t d) -> p t d", d=128),
                          xsrc[row0:row0 + 512, db * 128:(db + 1) * 128].rearrange("(t p) d -> p t d", p=128),
                          reads=[XK], writes=[('o_xr', db % 2)])
                    ys = yst[db % 2]
                    S.op('dve', lambda e: e.tensor_tensor(out=ys, in0=PS(b2), in1=xrt, op=ALU.add),
                         reads=[('ps', b2), ('o_xr', db % 2)], writes=[('o_ys', db % 2)])
                    S.dma('sp', xdst[row0:row0 + 512, db * 128:(db + 1) * 128].rearrange("(t p) d -> p t d", p=128),
                          ys.rearrange("p (t d) -> p t d", d=128), reads=[('o_ys', db % 2)], writes=[('yout', row0, db)])
                WS.add(w_o[L, db], 32, fo)
            WS.run()


        passes = [
            dict(tiles=[0, 1, 2, 3], ext=4, ext_after=True, segs=[0, 1], mrow=1, group=True, kv=True),
            dict(tiles=[4, 5, 6, 7], ext=3, ext_after=False, segs=[2, 3], mrow=1, group=True, kv=False),
            dict(tiles=[8, 9, 10, 11], ext=None, ext_after=True, segs=[4, 5], mrow=0, group=False, kv=True),
        ]
        for pi, P in enumerate(passes):
            P['pi'] = pi
            P['tok0'] = P['tiles'][0] * 128
            if P['group']:
                P['nkt'] = 12
                P['krow0'] = 0
                P['bias'] = (lambda kt, s, pi=pi: abg[:, (2 * pi + s) * 12 + kt:(2 * pi + s) * 12 + kt + 1])
            else:
                P['nkt'] = 4
                P['krow0'] = 512 + P['tok0']
                P['bias'] = (lambda kt, s: abs_[:, s * 4 + kt:s * 4 + kt + 1])

        stage('mod')
        cache_step()
        S.barrier()
        stage('cache')
        P1 = passes[1]
        norm_phase(P1['tiles'], P1['ext'], P1['mrow'])
        stage('norm')
        kv_step(P1['tiles'], P1['segs'])
        S.barrier()
        stage('kv')
        if 1 in branches:
            branch_ssd(P1, 'pre')
            S.barrier()
        if 2 in branches:
            branch_hgrn(P1, 'pre')
            S.barrier()
        stage('pre')
        for pi, P in enumerate(passes):
            if pi not in cfg.get('passes', (0, 1, 2)):
                continue
            norm_phase(P['tiles'], P['ext'], P['mrow'])
            if P['kv']:
                kv_step(P['tiles'], P['segs'])
                S.barrier()
            first = True
            for br in branches:
                if br == 0:
                    branch_attn(P)
                elif br == 1:
                    branch_ssd(P, 'full')
                else:
                    branch_hgrn(P, 'full')
                S.barrier()
                stage('attn')
                branch_merge(br)
                S.barrier()
                stage('merge')
            out_phase(P['tiles'], P['mrow'])
            S.barrier()
            stage('out')

    with nc.allow_non_contiguous_dma(reason="small strided parameter / layout loads"):
        try:
            if cfg.get('stop') != 'consts':
                for L in range(nlayers):
                    layer(L)
        except (Stop, StopBuild):
            pass
        S.maxops = None
        S.finish()
    return nc, S


def _consts():
    p = np.arange(128)[:, None]
    f = np.arange(128)[None, :]
    c = np.zeros((128, C_END), np.float32)
    c[:, C_ID:C_ID + 128] = (p == f)
    c[:, C_ONE:C_ONE + 128] = 1.0
    c[:, C_MF:C_MF + 128] = (p <= f)
    c[:, C_MB:C_MB + 128] = (p >= f)
    c[:, C_SLF:C_SLF + 128] = (p > f)
    c[:, C_SUB:C_SUB + 128] = (p < f)
    rm = np.zeros((128, 128), np.float32)
    for base in (0, 64):
        for m in range(32):
            rm[base + 32 + m, base + m] = -1.0
            rm[base + m, base + 32 + m] = 1.0
    c[:, C_RM:C_RM + 128] = rm
    same = (p // 64) == (f // 64)
    c[:, C_BDF:C_BDF + 128] = (p <= f) & same
    c[:, C_BDB:C_BDB + 128] = (p >= f) & same
    t = np.arange(512)
    c[:, C_RST:C_RST + 512] = (t % 64 != 0).astype(np.float32)[None, :]
    return c


def _rope_tables(is_sample):
    cos = np.ones((128, NTOK), np.float32)
    sin = np.zeros((128, NTOK), np.float32)
    if is_sample:
        t = np.arange(1024)
        row = (t // 64).astype(np.float32)
        col = (t % 64).astype(np.float32)
        half = 64
        inv = (10000.0 ** (-np.arange(0, half, 2, dtype=np.float32) / half)).astype(np.float32)
        ra = row[None, :] * inv[:, None]
        ca = col[None, :] * inv[:, None]
        ang = np.concatenate([ra, ra, ca, ca], axis=0).astype(np.float32)
        cos[:, :1024] = np.cos(ang)
        sin[:, :1024] = np.sin(ang)
    return cos, sin


def _tile_w(W):
    K, N = W.shape
    return np.ascontiguousarray(W.reshape(K // 128, 128, N // 128, 128).transpose(2, 1, 0, 3)).reshape(N // 128, 128, K)


def _prep_weights(inp, nl=DEPTH):
    wt = {}
    w_in = inp['w_in']
    wt['w_dt'] = np.ascontiguousarray(w_in[:nl, :, BDT0:BDT0 + 64])
    wt['w_in'] = np.stack([_tile_w(np.concatenate([w_in[l][:, :BDT0], w_in[l][:, BDT0 + 64:]], axis=1)) for l in range(nl)])
    wt['w_mod'] = np.stack([_tile_w(inp['w_mod'][l]) for l in range(nl)])
    wt['w_bout'] = np.stack([np.stack([_tile_w(inp['w_bout'][l, i]) for i in range(3)]) for l in range(nl)])
    wt['w_o'] = np.stack([_tile_w(inp['w_o'][l]) for l in range(nl)])
    return wt


def _core_inputs(i, inp, wt):
    is_sample = i < 4
    d = {}
    if is_sample:
        j = i
        segs_x = [inp['x_sample'][j].reshape(4, SEGL, D)[s] for s in range(4)] + [inp['x_prompt'][2 * j], inp['x_prompt'][2 * j + 1]]
        c1 = inp['c'][j]
        ck = np.ascontiguousarray(np.transpose(inp['cache_k'][j], (0, 2, 3, 1)))
        cv = np.ascontiguousarray(inp['cache_v'][j].reshape(DEPTH, 512, 512))
        ssm = inp['state_ssm'][j]
        ssm = np.ascontiguousarray(np.transpose(ssm.reshape(DEPTH, 2, 4, 512, 128), (0, 1, 2, 4, 3)))
        hg = np.ascontiguousarray(inp['state_hgrn'][j])
    else:
        base = 8 + 6 * (i - 4)
        segs_x = [inp['x_prompt'][base + s] for s in range(6)]
        c1 = inp['c_ctx']
        ck = np.zeros((DEPTH, 4, 128, 512), np.float32)
        cv = np.zeros((DEPTH, 512, 512), np.float32)
        ssm = np.zeros((DEPTH, 2, 4, 128, 512), np.float32)
        hg = np.zeros((DEPTH, 2, 16, 128, 128), np.float32)
    d['x'] = np.ascontiguousarray(np.concatenate(segs_x, axis=0))
    d['c2'] = np.ascontiguousarray(np.stack([inp['c_ctx'], c1], axis=0))
    d['cache_k'] = ck
    d['cache_v'] = cv
    d['st_ssm'] = ssm
    d['st_hg'] = hg
    d['cst'] = _consts()
    cos, sin = _rope_tables(is_sample)
    d['ropec'] = cos
    d['ropes'] = sin
    ab = np.zeros((4, 12), np.float32)
    if not is_sample:
        ab[:] = NEG
        for a in range(4):
            ab[a, 4 + 2 * a] = 0.0
            ab[a, 5 + 2 * a] = 0.0
    d['abias_g'] = np.ascontiguousarray(np.broadcast_to(ab.reshape(1, 48), (128, 48))).astype(np.float32)
    ab2 = np.full((2, 4), NEG, np.float32)
    for s in range(2):
        ab2[s, 2 * s] = 0.0
        ab2[s, 2 * s + 1] = 0.0
    d['abias_s'] = np.ascontiguousarray(np.broadcast_to(ab2.reshape(1, 8), (128, 8))).astype(np.float32)
    fl = np.zeros((128, 4), np.float32)
    fl[:, 0] = 1.0 if is_sample else 0.0
    fl[:, 2] = 1.0
    d['flags'] = fl
    for k in ('ln_w', 'b_mod', 'q_norm_w', 'k_norm_w', 'conv_w', 'conv_b', 'd_skip',
              'ssm_norm_w', 'hgrn_lb', 'hgrn_norm_w'):
        d[k] = np.ascontiguousarray(inp[k])
    d.update(wt)
    d['dt_bias'] = np.ascontiguousarray(inp['dt_bias'].reshape(DEPTH, 64))
    d['a_log'] = np.ascontiguousarray(inp['a_log'].reshape(DEPTH, 64))
    return d


def _assemble(results):
    y_prompt = np.zeros((32, SEGL, D), np.float32)
    y_sample = np.zeros((4, 1024, D), np.float32)
    nck = np.zeros((32, DEPTH, SEGL, 4, 128), np.float32)
    ncv = np.zeros((32, DEPTH, SEGL, 4, 128), np.float32)
    nss = np.zeros((32, DEPTH, 2, 32, 64, 128), np.float32)
    nsh = np.zeros((32, DEPTH, 2, 16, 128, 128), np.float32)
    for i, r in enumerate(results):
        y = r['y'].reshape(NSEG, SEGL, D)
        if i < 4:
            y_sample[i] = y[0:4].reshape(1024, D)
            pmap = {4: 2 * i, 5: 2 * i + 1}
        else:
            pmap = {s: 8 + 6 * (i - 4) + s for s in range(6)}
        for s, b in pmap.items():
            y_prompt[b] = y[s]
            nck[b] = np.transpose(r['ck'][s], (0, 3, 1, 2))
            ncv[b] = r['cv'][s].reshape(DEPTH, SEGL, 4, 128)
            ss = r['ss'][s]
            nss[b] = np.transpose(ss, (0, 1, 2, 4, 3)).reshape(DEPTH, 2, 32, 64, 128)
            nsh[b] = r['sh'][s]
    return (y_prompt, y_sample, nck, ncv, nss, nsh)


_CACHE = {}


def kernel(**inputs):
    inp = {k: np.asarray(v) for k, v in inputs.items()}
    if 'nc' not in _CACHE:
        _CACHE['nc'] = build_program({})[0]
    nc = _CACHE['nc']
    wt = _prep_weights(inp)
    in_maps = [_core_inputs(i, inp, wt) for i in range(NCORE)]
    res = run_bass_kernel_spmd(nc, in_maps, core_ids=list(range(NCORE)))
    return _assemble(res.results)
```
